# Optimizing a Trainium2 kernel written in Bass

```python
import math
import jax, jax.numpy as jnp
from jax import lax
import numpy as np

D_MODEL = 1024
BATCH = 8
SEQ = 8192
DEPTH = 2

N_MIXERS = 2
N_FOURIER_GROUPS = 8
FOURIER_GROUP = D_MODEL // N_FOURIER_GROUPS
N_DIFF_HEADS = 8
DIFF_HEAD_DIM = D_MODEL // (2 * N_DIFF_HEADS)
DIFF_V_DIM = 2 * DIFF_HEAD_DIM
D_FF = -(-8 * D_MODEL // (3 * 256)) * 256
Q_BLOCK = 128
RMS_EPS = 1e-6
N_FOURIER_LAYERS = (DEPTH + 1) // 2
N_DIFF_LAYERS = DEPTH // 2

kernel_name = "hybrid_fnet_diffattn_alibi_encoder"


def rms_norm(x, g):
    xf = x.astype(jnp.float32)
    y = xf * lax.rsqrt(jnp.mean(xf * xf, axis=-1, keepdims=True) + RMS_EPS)
    return (y * g.astype(jnp.float32)).astype(x.dtype)


def alibi_slopes(n_heads):
    return jnp.asarray([2.0 ** (-8.0 * (h + 1) / n_heads) for h in range(n_heads)], dtype=jnp.float32)


def lambda_init_fn(layer_idx):
    return 0.8 - 0.6 * math.exp(-0.3 * layer_idx)


def fourier_mixer(xn, w_o):
    b, s, d = xn.shape
    xg = xn.reshape(b, s, N_FOURIER_GROUPS, FOURIER_GROUP).astype(jnp.float32)
    y = jnp.fft.fftn(xg, axes=(1, 3), norm="ortho").real
    y = y.reshape(b, s, d).astype(xn.dtype)
    return y @ w_o


def diff_attention_mixer(xn, w_qkv, lq1, lk1, lq2, lk2, subln_g, w_o, layer_idx):
    b, s, d = xn.shape
    h, dh, dv = N_DIFF_HEADS, DIFF_HEAD_DIM, DIFF_V_DIM
    qkv = xn @ w_qkv
    q = qkv[..., :d].reshape(b, s, h, 2, dh)
    k = qkv[..., d:2 * d].reshape(b, s, h, 2, dh)
    v = qkv[..., 2 * d:].reshape(b, s, h, dv)

    lam_init = lambda_init_fn(layer_idx)
    lam = (jnp.exp(jnp.sum(lq1.astype(jnp.float32) * lk1.astype(jnp.float32)))
           - jnp.exp(jnp.sum(lq2.astype(jnp.float32) * lk2.astype(jnp.float32)))
           + lam_init)

    scale = dh ** -0.5
    n_blk = s // Q_BLOCK
    qb = q.reshape(b, n_blk, Q_BLOCK, h, 2, dh).transpose(1, 0, 3, 4, 2, 5)
    kt = k.transpose(0, 2, 3, 1, 4)
    vt = v.transpose(0, 2, 1, 3)
    slopes = alibi_slopes(h)
    key_pos = jnp.arange(s, dtype=jnp.int32)
    starts = jnp.arange(n_blk, dtype=jnp.int32) * Q_BLOCK

    def block(args):
        q_blk, t0 = args
        scores = jnp.einsum('bhcqd,bhcsd->bhcqs', q_blk, kt).astype(jnp.float32) * scale
        q_pos = t0 + jnp.arange(Q_BLOCK, dtype=jnp.int32)
        dist = jnp.abs(q_pos[:, None] - key_pos[None, :]).astype(jnp.float32)
        bias = -slopes[:, None, None] * dist[None]
        p = jax.nn.softmax(scores + bias[None, :, None], axis=-1)
        a = p[:, :, 0] - lam * p[:, :, 1]
        return jnp.einsum('bhqs,bhse->bhqe', a.astype(vt.dtype), vt)

    o = lax.map(block, (qb, starts))
    o = o.transpose(1, 0, 3, 2, 4).reshape(b, s, h, dv)
    o = rms_norm(o, subln_g) * (1.0 - lam_init)
    return o.reshape(b, s, h * dv) @ w_o


def swiglu_ffn(xn, w_gate, w_up, w_down):
    return (jax.nn.silu(xn @ w_gate) * (xn @ w_up)) @ w_down


def setup_inputs(seed: int = 0) -> dict:
    key = jax.random.key(seed)
    ks = jax.random.split(key, 20)
    D, F = D_MODEL, D_FF

    def gain(k, shape):
        return 1.0 + 0.02 * jax.random.normal(k, shape, jnp.float32)

    def w(k, shape, fan_in):
        return jax.random.normal(k, shape, jnp.float32) * fan_in ** -0.5

    return {
        "x": jax.random.normal(ks[0], (BATCH, SEQ, D), jnp.float32),
        "norm_mix_pre": gain(ks[1], (DEPTH, D)),
        "norm_mix_post": gain(ks[2], (DEPTH, D)),
        "norm_ffn_pre": gain(ks[3], (DEPTH, D)),
        "norm_ffn_post": gain(ks[4], (DEPTH, D)),
        "fourier_w_o": w(ks[5], (N_FOURIER_LAYERS, D, D), D),
        "diff_w_qkv": w(ks[6], (N_DIFF_LAYERS, D, 3 * D), D),
        "diff_lambda_q1": 0.1 * jax.random.normal(ks[7], (N_DIFF_LAYERS, DIFF_HEAD_DIM), jnp.float32),
        "diff_lambda_k1": 0.1 * jax.random.normal(ks[8], (N_DIFF_LAYERS, DIFF_HEAD_DIM), jnp.float32),
        "diff_lambda_q2": 0.1 * jax.random.normal(ks[9], (N_DIFF_LAYERS, DIFF_HEAD_DIM), jnp.float32),
        "diff_lambda_k2": 0.1 * jax.random.normal(ks[10], (N_DIFF_LAYERS, DIFF_HEAD_DIM), jnp.float32),
        "diff_subln_g": gain(ks[11], (N_DIFF_LAYERS, DIFF_V_DIM)),
        "diff_w_o": w(ks[12], (N_DIFF_LAYERS, D, D), D),
        "ffn_w_gate": w(ks[13], (DEPTH, D, F), D),
        "ffn_w_up": w(ks[14], (DEPTH, D, F), D),
        "ffn_w_down": w(ks[15], (DEPTH, F, D), F),
    }


def reference(x, norm_mix_pre, norm_mix_post, norm_ffn_pre, norm_ffn_post,
              fourier_w_o, diff_w_qkv, diff_lambda_q1, diff_lambda_k1,
              diff_lambda_q2, diff_lambda_k2, diff_subln_g, diff_w_o,
              ffn_w_gate, ffn_w_up, ffn_w_down):
    h = x
    for i in range(DEPTH):
        xn = rms_norm(h, norm_mix_pre[i])
        j = i // N_MIXERS
        if i % N_MIXERS == 0:
            m = fourier_mixer(xn, fourier_w_o[j])
        else:
            m = diff_attention_mixer(xn, diff_w_qkv[j], diff_lambda_q1[j], diff_lambda_k1[j],
                                     diff_lambda_q2[j], diff_lambda_k2[j], diff_subln_g[j],
                                     diff_w_o[j], i)
        h = h + rms_norm(m, norm_mix_post[i])
        f = swiglu_ffn(rms_norm(h, norm_ffn_pre[i]), ffn_w_gate[i], ffn_w_up[i], ffn_w_down[i])
        h = h + rms_norm(f, norm_ffn_post[i])
    return h
```

```python
import math
import numpy as np
import ml_dtypes
import concourse.bass as bass
import concourse.mybir as mybir
from concourse.bass_utils import run_bass_kernel_spmd

F32 = mybir.dt.float32
BF16 = mybir.dt.bfloat16
ALU = mybir.AluOpType
AF = mybir.ActivationFunctionType

S = 8192
D = 1024
FF = 2816
NJ = FF // 128
NH = 8
EPS = 1e-6
LAM_INIT = 0.8 - 0.6 * math.exp(-0.3 * 1)
SLOPES = [2.0 ** (-(h + 1)) for h in range(NH)]
VP = 144
ATT_THRESH = 48.0

SAME_ENGINE_SYNC = True
DMA_RING = 12
SEM_EPOCH = 16000
LOOKAHEAD = 4
STREAMS = ("pe", "act", "dve", "pool", "sp")


class Buf:
    __slots__ = ("lastw", "lastr")

    def __init__(self):
        self.lastw = {}
        self.lastr = {}


class Op:
    __slots__ = ("stream", "fn", "deps", "sig", "key", "key2", "val", "dma")


class Prog:
    def __init__(self, nc):
        self.nc = nc
        self.ops = {s: [] for s in STREAMS}
        self.dma_n = {s: 0 for s in STREAMS}
        self.ring_last = {}
        self.pending = {s: set() for s in STREAMS}
        self.last = {s: None for s in STREAMS}

    def add(self, stream, fn, reads=(), writes=(), dma=False):
        op = Op()
        op.stream, op.fn, op.dma, op.sig, op.val = stream, fn, dma, dma, None
        deps = set()
        for b in reads:
            deps.update(b.lastw.values())
        for b in writes:
            deps.update(b.lastw.values())
            deps.update(b.lastr.values())
        if dma:
            idx = self.dma_n[stream] % DMA_RING
            self.dma_n[stream] += 1
            op.key = (stream, idx)
            prev = self.ring_last.get(op.key)
            if prev is not None:
                deps.add(prev)
            self.ring_last[op.key] = op
        else:
            op.key = stream
        if self.pending[stream]:
            deps.update(self.pending[stream])
            self.pending[stream] = set()
        fdeps = []
        for d in deps:
            if (not d.dma) and (not dma) and d.stream == stream:
                if stream == "pe" or not SAME_ENGINE_SYNC:
                    continue
            d.sig = True
            fdeps.append(d)
        op.deps = fdeps
        for b in reads:
            b.lastr[op.key] = op
        for b in writes:
            b.lastw = {op.key: op}
            b.lastr = {}
        self.ops[stream].append(op)
        if not dma:
            self.last[stream] = op
        return op

    def barrier(self):
        allops = set(o for o in self.last.values() if o is not None)
        allops.update(self.ring_last.values())
        for s in STREAMS:
            self.pending[s] = set(allops)
        for o in allops:
            o.sig = True

    def emit(self):
        nc = self.nc
        sems = {}
        cnt = {}
        epoch = {}

        def sem(key):
            if key not in sems:
                sems[key] = nc.alloc_semaphore("s_" + "_".join(str(k) for k in key))
            return sems[key]

        for s_ in STREAMS:
            for op in self.ops[s_]:
                if op.dma:
                    op.key2 = ("d",) + op.key
                    cnt[op.key2] = cnt.get(op.key2, 0) + 16
                    op.val = cnt[op.key2]
                    sem(op.key2)
                elif op.sig:
                    ep_ = epoch.get(s_, 0)
                    k2 = ("c", s_, ep_)
                    if cnt.get(k2, 0) >= SEM_EPOCH:
                        ep_ += 1
                        epoch[s_] = ep_
                        k2 = ("c", s_, ep_)
                    op.key2 = k2
                    cnt[k2] = cnt.get(k2, 0) + 1
                    op.val = cnt[k2]
                    sem(k2)

        def replay(stream):
            def body(eng):
                waited = {}
                for op in self.ops[stream]:
                    for d in op.deps:
                        if waited.get(d.key2, 0) >= d.val:
                            continue
                        eng.wait_ge(sems[d.key2], d.val)
                        waited[d.key2] = d.val
                    inst = op.fn(eng)
                    if op.sig:
                        inst.then_inc(sems[op.key2], 16 if op.dma else 1)
                for key, v in cnt.items():
                    if key[0] == "d" and key[1] == stream and waited.get(key, 0) < v:
                        eng.wait_ge(sems[key], v)
            return body

        with nc.Block() as block:
            block.tensor(replay("pe"))
            block.scalar(replay("act"))
            block.vector(replay("dve"))
            block.gpsimd(replay("pool"))
            block.sync(replay("sp"))


class Arena:
    def __init__(self, nc, nbytes):
        self.ap = nc.alloc_sbuf_tensor("arena", [128, nbytes // 2], BF16).ap()
        self.cap = nbytes // 2
        self.off = 0

    def alloc(self, n, dt):
        k = n * 2 if dt == F32 else n
        k = (k + 15) // 16 * 16
        assert self.off + k <= self.cap, ("SBUF arena overflow", self.off, k, self.cap)
        a = self.ap[:, self.off:self.off + n * (2 if dt == F32 else 1)]
        self.off += k
        return a.bitcast(F32) if dt == F32 else a

    def tile(self, n, dt):
        return (self.alloc(n, dt), Buf())

    def ring(self, cnt, n, dt):
        return Ring([self.tile(n, dt) for _ in range(cnt)])


class Ring:
    def __init__(self, items):
        self.items = items
        self.i = 0

    def next(self):
        it = self.items[self.i % len(self.items)]
        self.i += 1
        return it


def _host_consts():
    bf = ml_dtypes.bfloat16
    c = {}
    c["ident"] = np.eye(128, dtype=np.float32).astype(bf)
    s1 = np.arange(64)
    k1 = np.arange(64)
    GA = np.zeros((128, 64, 2, 128), np.float32)
    for j in range(64):
        for e in range(2):
            s = 128 * s1 + 2 * j + e
            ang = 2 * np.pi * ((s[:, None] * k1[None, :]) % S) / S
            GA[e * 64:(e + 1) * 64, j, 0, e * 64:(e + 1) * 64] = np.cos(ang) / 32
            GA[e * 64:(e + 1) * 64, j, 1, e * 64:(e + 1) * 64] = -np.sin(ang) / 32
    c["ga"] = GA.reshape(128, 64 * 2 * 128).astype(bf)
    a = np.arange(128)
    ang = 2 * np.pi * ((a[:, None] * a[None, :]) % 128) / 128
    C, Sn = np.cos(ang), np.sin(ang)
    c["fc"] = np.concatenate([C / 32, -Sn / 32, Sn / 32, C / 32], axis=1).astype(bf)
    c["ch"] = np.concatenate([C, Sn], axis=1).astype(bf)
    qq = np.arange(512, dtype=np.float32)
    c["q1"] = np.broadcast_to(qq[None, :], (128, 512)).astype(np.float32).copy()
    kk = np.arange(128, dtype=np.float32)
    T1 = np.zeros((128, 4, 512), np.float32)
    for j in range(4):
        T1[:, j, :] = -np.abs(128 * j + kk[:, None] - qq[None, :])
    c["t1"] = T1.reshape(128, 2048)
    bc = np.zeros((128, NH, 2, 64), np.float32)
    n = np.arange(64, dtype=np.float32)
    for h in range(NH):
        sl = SLOPES[h]
        bc[:, h, 0, :] = -sl * kk[:, None] - sl * 128 * n[None, :]
        bc[:, h, 1, :] = sl * kk[:, None] - sl * 128 * n[None, :]
    c["bc"] = bc.reshape(128, NH * 2 * 64)
    return c


_CONSTS = None


def _layout_weights(inp):
    w = {}
    f = lambda a: np.ascontiguousarray(a, dtype=np.float32)
    wg = np.asarray(inp["ffn_w_gate"]); wu = np.asarray(inp["ffn_w_up"]); wd = np.asarray(inp["ffn_w_down"])
    w["wg"] = f(wg.reshape(2, 8, 128, NJ, 128).transpose(0, 3, 2, 1, 4)).reshape(2 * NJ * 128, 1024)
    w["wu"] = f(wu.reshape(2, 8, 128, NJ, 128).transpose(0, 3, 2, 1, 4)).reshape(2 * NJ * 128, 1024)
    w["wd"] = f(wd.reshape(2, NJ, 128, 2, 512).transpose(0, 3, 2, 1, 4)).reshape(2 * 2 * 128, NJ * 512)
    wqkv = np.asarray(inp["diff_w_qkv"])[0]
    w["wqk"] = f(wqkv[:, :2048].reshape(8, 128, 16, 128).transpose(2, 1, 0, 3)).reshape(16 * 128, 1024)
    w["wv"] = f(wqkv[:, 2048:].reshape(8, 128, 1024).transpose(1, 0, 2)).reshape(128, 8192)
    w["wo"] = f(np.asarray(inp["diff_w_o"])[0].reshape(8, 128, 1024).transpose(1, 0, 2)).reshape(128, 8192)
    w["wf"] = f(np.asarray(inp["fourier_w_o"])[0].reshape(8, 128, 1024).transpose(1, 0, 2)).reshape(128, 8192)
    return w


def build(mode="full"):
    nc = bass.Bass("TRN2", target_bir_lowering=False)
    P = Prog(nc)
    do0 = mode in ("full", "L0")
    do1 = mode in ("full", "L1")

    def din(name, shape, dt=F32):
        return nc.dram_tensor(name, list(shape), dt, kind="ExternalInput").ap()

    def dscr(name, shape, dt):
        return nc.dram_tensor(name, list(shape), dt).ap()

    x = din("x", [S, D]) if do0 else None
    if mode == "L0":
        h1 = nc.dram_tensor("h1", [S, D], F32, kind="ExternalOutput").ap()
    elif mode == "L1":
        h1 = din("h1", [S, D])
    else:
        h1 = dscr("h1", [S, D], F32)
    out = nc.dram_tensor("out", [S, D], F32, kind="ExternalOutput").ap() if do1 else None
    g_mix_pre = din("norm_mix_pre", [2, D]); g_mix_post = din("norm_mix_post", [2, D])
    g_ffn_pre = din("norm_ffn_pre", [2, D]); g_ffn_post = din("norm_ffn_post", [2, D])
    lam_in = [din(n, [1, 64]) for n in ("diff_lambda_q1", "diff_lambda_k1", "diff_lambda_q2", "diff_lambda_k2")]
    subln = din("diff_subln_g", [1, 128])
    ident_d = din("ident", [128, 128], BF16)
    ga_d = din("ga", [128, 64 * 2 * 128], BF16)
    fc_d = din("fc", [128, 512], BF16)
    ch_d = din("ch", [128, 256], BF16)
    q1_d = din("q1", [128, 512]); t1_d = din("t1", [128, 2048]); bc_d = din("bc", [128, NH * 2 * 64])
    wsrc = {"wg": din("wg", [2 * NJ * 128, 1024]), "wu": din("wu", [2 * NJ * 128, 1024]),
            "wd": din("wd", [2 * 2 * 128, NJ * 512]), "wqk": din("wqk", [16 * 128, 1024]),
            "wv": din("wv", [128, 8192]), "wo": din("wo", [128, 8192]), "wf": din("wf", [128, 8192])}
    wb = {k: dscr(k + "_b", v.shape, BF16) for k, v in wsrc.items()}
    wbuf = {}

    cast_q = []

    def cast(key, r0, r1, c0, c1):
        b = Buf()
        wbuf.setdefault(key, []).append(((r0, r1, c0, c1), b))
        src = wsrc[key][r0:r1, c0:c1]; dst = wb[key][r0:r1, c0:c1]
        cast_q.append((src, dst, b))

    def pump(n):
        for _ in range(min(n, len(cast_q))):
            src, dst, b = cast_q.pop(0)
            P.add("pool", lambda e, src=src, dst=dst: e.dma_start(out=dst, in_=src), writes=[b], dma=True)

    def wdeps(key, r0, r1, c0=None, c1=None):
        res = []
        for (a0, a1, b0, b1), b in wbuf[key]:
            if a0 < r1 and r0 < a1 and (c0 is None or (b0 < c1 and c0 < b1)):
                res.append(b)
        return res

    def cast_ffn(l):
        for j in range(NJ):
            r = (l * NJ + j) * 128
            cast("wg", r, r + 128, 0, 1024)
            cast("wu", r, r + 128, 0, 1024)
        for e in range(2):
            r = (l * 2 + e) * 128
            for q in range(4):
                cast("wd", r, r + 128, q * 2816, (q + 1) * 2816)

    if do0:
        for q in range(4):
            cast("wf", 0, 128, q * 2048, (q + 1) * 2048)
        cast_ffn(0)
    n_l0_casts = len(cast_q)
    if do1:
        for m in range(16):
            cast("wqk", m * 128, (m + 1) * 128, 0, 1024)
        for q in range(4):
            cast("wv", 0, 128, q * 2048, (q + 1) * 2048)
        for q in range(4):
            cast("wo", 0, 128, q * 2048, (q + 1) * 2048)
        cast_ffn(1)
    if not do0:
        pump(10 ** 6)
    else:
        pump(8)

    if do0:
        ar_d = dscr("ar_s", [64, 128, D], BF16); ai_d = dscr("ai_s", [64, 128, D], BF16)
        a_bufs = [Buf() for _ in range(64)]
    if do1:
        qT_d = dscr("qT_s", [NH, 128, S], BF16); kT_d = dscr("kT_s", [NH, 128, S], BF16)
        v_d = dscr("v_s", [NH, 128, 64, 128], BF16)
        o_d = dscr("o_s", [S, D], BF16)
        qkv_bufs = [Buf() for _ in range(16)]
        o_bufs = [Buf() for _ in range(16)]
    h1_bufs = [Buf() for _ in range(16)]

    A = Arena(nc, 207 * 1024)
    psum_all = nc.alloc_psum_tensor("psum_all", [128, 4096], F32).ap()
    banks = [(psum_all[:, i * 512:(i + 1) * 512], Buf()) for i in range(8)]
    ident = A.tile(128, BF16)
    cst = A.tile(8, F32)
    P.add("sp", lambda e: e.dma_start(out=ident[0], in_=ident_d), writes=[ident[1]], dma=True)
    P.add("pool", lambda e: e.memset(cst[0][:, 0:4], EPS), writes=[cst[1]])
    P.add("pool", lambda e: e.memset(cst[0][:, 4:8], -0.5), writes=[cst[1]])
    persist_mark = A.off

    def load_gain(g_d, i):
        t = A.tile(D, F32)
        src = g_d[i:i + 1, :].partition_broadcast(128)
        P.add("sp", lambda e: e.dma_start(out=t[0], in_=src), writes=[t[1]], dma=True)
        return t

    def rstd_from(msq, ncols, width_tag=None):
        ap, b = msq
        col = 0
        if ncols == 2:
            P.add("pool", lambda e: e.tensor_tensor(out=ap[:, 2:3], in0=ap[:, 0:1], in1=ap[:, 1:2], op=ALU.add),
                  reads=[b], writes=[b])
            col = 2
        P.add("pool", lambda e: e.tensor_tensor(out=ap[:, 3:4], in0=ap[:, col:col + 1], in1=cst[0][:, 0:1], op=ALU.add),
              reads=[b, cst[1]], writes=[b])
        P.add("pool", lambda e: e.tensor_tensor(out=ap[:, 4:5], in0=ap[:, 3:4], in1=cst[0][:, 4:5], op=ALU.pow),
              reads=[b, cst[1]], writes=[b])
        return ap[:, 4:5]

    def rstd4(msq):
        ap, b = msq
        P.add("pool", lambda e: e.tensor_tensor(out=ap[:, 4:8], in0=ap[:, 0:4], in1=cst[0][:, 0:4], op=ALU.add),
              reads=[b, cst[1]], writes=[b])
        P.add("pool", lambda e: e.tensor_tensor(out=ap[:, 4:8], in0=ap[:, 4:8], in1=cst[0][:, 4:8], op=ALU.pow),
              reads=[b, cst[1]], writes=[b])

    evac_rr = [0]

    def evac(dst, dstb, src, srcb, extra_reads=()):
        evac_rr[0] += 1
        if evac_rr[0] % 2:
            P.add("act", lambda e: e.activation(out=dst, in_=src, func=AF.Copy), reads=[srcb] + list(extra_reads), writes=[dstb])
        else:
            P.add("dve", lambda e: e.tensor_copy(out=dst, in_=src), reads=[srcb] + list(extra_reads), writes=[dstb])

    def mm(out, outb, lhsT, rhs, start, stop, reads, skip=False):
        if skip:
            P.add("pe", lambda e: e.matmul(out, lhsT=lhsT, rhs=rhs, start=start, stop=stop, skip_group_check=True),
                  reads=reads, writes=[outb])
        else:
            P.add("pe", lambda e: e.matmul(out, lhsT=lhsT, rhs=rhs, start=start, stop=stop), reads=reads, writes=[outb])

    def norm_part(H, gain, xb_ring, msq_ring, junk):
        Hap, Hb = H
        msq = msq_ring.next()
        for t in range(4):
            hs = Hap[:, t * D:(t + 1) * D]
            P.add("act", lambda e, hs=hs, t=t: e.activation(out=junk[0], in_=hs, func=AF.Square, scale=1.0 / 32,
                                                             accum_out=msq[0][:, t:t + 1]),
                  reads=[Hb], writes=[junk[1], msq[1]])
        rstd4(msq)
        xbs = []
        for t in range(4):
            hs = Hap[:, t * D:(t + 1) * D]
            xb = xb_ring.next()
            xbs.append(xb)
            P.add("dve", lambda e, hs=hs, xb=xb, t=t: e.scalar_tensor_tensor(out=xb[0], in0=hs, scalar=msq[0][:, 4 + t:5 + t], in1=gain[0],
                                                                            op0=ALU.mult, op1=ALU.mult),
                  reads=[Hb, msq[1], gain[1]], writes=[xb[1]])
        return xbs

    def transpose_part(xbs, xnT, misc_banks):
        for t in range(4):
            xb = xbs[t]
            for half in range(2):
                bk = misc_banks.next()
                for q in range(4):
                    i = half * 4 + q
                    mm(bk[0][:, q * 128:(q + 1) * 128], bk[1], xb[0][:, i * 128:(i + 1) * 128], ident[0], True, True,
                       [xb[1], ident[1]])
                dst = xnT[0].rearrange("p (i k) -> p i k", k=512)[:, half * 4:half * 4 + 4, t * 128:(t + 1) * 128]
                src = bk[0].rearrange("p (q k) -> p q k", k=128)
                evac(dst, xnT[1], src, bk[1])

    def norm_transpose(H, gain, xb_ring, xnT, msq_ring, junk, misc_banks):
        transpose_part(norm_part(H, gain, xb_ring, msq_ring, junk), xnT, misc_banks)

    def post_norm_add(src_halves, gain, Hap_t, Hb, msq_ring, junk, tsb):
        msq = msq_ring.next()
        for e_, (sa, sb_) in enumerate(src_halves):
            P.add("act", lambda e, sa=sa, e_=e_: e.activation(out=junk[0][:, 0:512], in_=sa, func=AF.Square, scale=1.0 / 32,
                                                              accum_out=msq[0][:, e_:e_ + 1]),
                  reads=[sb_], writes=[junk[1], msq[1]])
        rs = rstd_from(msq, 2)
        for e_, (sa, sb_) in enumerate(src_halves):
            P.add("dve", lambda e, sa=sa, e_=e_: e.scalar_tensor_tensor(out=tsb[0][:, e_ * 512:(e_ + 1) * 512], in0=sa, scalar=rs,
                                                                        in1=gain[0][:, e_ * 512:(e_ + 1) * 512],
                                                                        op0=ALU.mult, op1=ALU.mult),
                  reads=[sb_, msq[1], gain[1]], writes=[tsb[1]])
        P.add("pool", lambda e: e.tensor_tensor(out=Hap_t, in0=Hap_t, in1=tsb[0], op=ALU.add), reads=[tsb[1], Hb], writes=[Hb])

    def make_ffn_state():
        st = {}
        st["xb"] = A.ring(4, D, BF16)
        st["xnT"] = A.tile(8 * 512, BF16)
        st["msq"] = A.ring(4, 8, F32)
        st["junk"] = A.tile(D, BF16)
        st["tsb"] = A.tile(D, F32)
        st["wg"] = A.ring(3, 1024, BF16)
        st["wu"] = A.ring(3, 1024, BF16)
        st["wd"] = [A.tile(NJ * 512, BF16) for _ in range(2)]
        st["sg"] = A.ring(2, 512, F32)
        st["act"] = A.tile(NJ * 512, BF16)
        st["fsb"] = A.ring(1, D, F32)
        st["gbanks"] = Ring([banks[0], banks[1]])
        st["ubanks"] = Ring([banks[2], banks[3]])
        st["dbanks"] = Ring([banks[4], banks[5]])
        st["misc"] = Ring([banks[6], banks[7]])
        st["mbanks"] = Ring(banks[0:6])
        return st

    def ffn_wload(st, l, j):
        r = (l * NJ + j) * 128
        wg_t = st["wg"].next(); wu_t = st["wu"].next()
        P.add("sp", lambda e, wg_t=wg_t, r=r: e.dma_start(out=wg_t[0], in_=wb["wg"][r:r + 128, :]),
              reads=wdeps("wg", r, r + 128), writes=[wg_t[1]], dma=True)
        P.add("sp", lambda e, wu_t=wu_t, r=r: e.dma_start(out=wu_t[0], in_=wb["wu"][r:r + 128, :]),
              reads=wdeps("wu", r, r + 128), writes=[wu_t[1]], dma=True)
        return (wg_t, wu_t)

    def ffn_gateup(st, l, prefetch_next, hook=None):
        xnT, act_t = st["xnT"], st["act"]
        pref = st.setdefault("pref", [])
        for j in range(NJ):
            if j == 6 and hook is not None:
                hook()
            wg_t, wu_t = pref.pop(0) if pref else ffn_wload(st, l, j)
            gb = st["gbanks"].next(); ub = st["ubanks"].next()
            for i in range(8):
                mm(gb[0], gb[1], wg_t[0][:, i * 128:(i + 1) * 128], xnT[0][:, i * 512:(i + 1) * 512], i == 0, i == 7,
                   [wg_t[1], xnT[1]])
            for i in range(8):
                mm(ub[0], ub[1], wu_t[0][:, i * 128:(i + 1) * 128], xnT[0][:, i * 512:(i + 1) * 512], i == 0, i == 7,
                   [wu_t[1], xnT[1]])
            sg = st["sg"].next()
            P.add("act", lambda e, sg=sg, gb=gb: e.activation(out=sg[0], in_=gb[0], func=AF.Silu), reads=[gb[1]], writes=[sg[1]])
            P.add("dve", lambda e, sg=sg, ub=ub, j=j: e.tensor_tensor(out=act_t[0][:, j * 512:(j + 1) * 512], in0=sg[0], in1=ub[0],
                                                                      op=ALU.mult),
                  reads=[sg[1], ub[1]], writes=[act_t[1]])
        for e_ in range(2):
            r = (l * 2 + e_) * 128
            wd_t = st["wd"][e_]
            P.add("sp", lambda e, wd_t=wd_t, r=r: e.dma_start(out=wd_t[0], in_=wb["wd"][r:r + 128, :]),
                  reads=wdeps("wd", r, r + 128), writes=[wd_t[1]], dma=True)
        if prefetch_next:
            for j in range(3):
                pref.append(ffn_wload(st, l, j))

    def ffn_down(st, H, g_post, ts):
        Hap, Hb = H
        act_t = st["act"]
        for t in ts:
            fsb = st["fsb"].next()
            for e_ in range(2):
                db = st["dbanks"].next()
                wd_t = st["wd"][e_]
                for j in range(NJ):
                    mm(db[0], db[1], act_t[0][:, j * 512 + t * 128:j * 512 + (t + 1) * 128], wd_t[0][:, j * 512:(j + 1) * 512],
                       j == 0, j == NJ - 1, [act_t[1], wd_t[1]])
                evac(fsb[0][:, e_ * 512:(e_ + 1) * 512], fsb[1], db[0], db[1])
            post_norm_add([(fsb[0][:, 0:512], fsb[1]), (fsb[0][:, 512:1024], fsb[1])], g_post,
                          Hap[:, t * D:(t + 1) * D], Hb, st["msq"], st["junk"], st["tsb"])

    def pipelined_blocks(st, l, f_loads, f_compute, g_pre, g_post, store):
        H = f_compute(f_loads(0))
        transpose_part(norm_part(H, g_pre, st["xb"], st["msq"], st["junk"]), st["xnT"], st["misc"])
        for blk in range(16):
            pump(6)
            nxt = []
            hook = (lambda blk=blk, nxt=nxt: nxt.append(f_loads(blk + 1))) if blk + 1 < 16 else None
            ffn_gateup(st, l, blk + 1 < 16, hook)
            if blk + 1 < 16:
                Hn = f_compute(nxt[0])
                xbs = norm_part(Hn, g_pre, st["xb"], st["msq"], st["junk"])
            ffn_down(st, H, g_post, (0, 1))
            if blk + 1 < 16:
                transpose_part(xbs, st["xnT"], st["misc"])
            ffn_down(st, H, g_post, (2, 3))
            store(H, blk)
            if blk + 1 < 16:
                H = Hn

    if do0:
        A.off = persist_mark
        ga = A.tile(64 * 2 * 128, BF16)
        P.add("sp", lambda e: e.dma_start(out=ga[0][:, 0:8192], in_=ga_d[:, 0:8192]), writes=[ga[1]], dma=True)
        P.add("sp", lambda e: e.dma_start(out=ga[0][:, 8192:16384], in_=ga_d[:, 8192:16384]), writes=[ga[1]], dma=True)
        gpre0 = load_gain(g_mix_pre, 0)
        xr = A.ring(3, D, F32)
        xbr = A.ring(2, D, BF16)
        msqr = A.ring(4, 8, F32)
        junk = A.tile(D, BF16)
        arr = A.ring(2, D, BF16); air = A.ring(2, D, BF16)
        x3 = x.rearrange("(a b) c -> a b c", b=128)
        bankr = Ring(banks)

        def a_load(j):
            xt = xr.next()
            for e_ in range(2):
                src = x3[:, 2 * j + e_, :]
                P.add("sp", lambda e, xt=xt, e_=e_, src=src: e.dma_start(out=xt[0][e_ * 64:(e_ + 1) * 64, :], in_=src),
                      writes=[xt[1]], dma=True)
            return xt

        n_rest = (len(cast_q) + 8) - n_l0_casts if do1 else 0
        n_rest = max(0, min(n_rest, len(cast_q)))
        pend = [a_load(0), a_load(1)]
        for j in range(64):
            xt = pend.pop(0)
            if len(cast_q) > n_rest:
                pump(min(3, len(cast_q) - n_rest))
            if j + 2 < 64:
                pend.append(a_load(j + 2))
            msq = msqr.next()
            P.add("act", lambda e, xt=xt, msq=msq, junk=junk: e.activation(out=junk[0], in_=xt[0], func=AF.Square, scale=1.0 / 32,
                                                                           accum_out=msq[0][:, 0:1]),
                  reads=[xt[1]], writes=[junk[1], msq[1]])
            rs = rstd_from(msq, 1)
            xb = xbr.next()
            P.add("dve", lambda e, xt=xt, rs=rs, xb=xb: e.scalar_tensor_tensor(out=xb[0], in0=xt[0], scalar=rs, in1=gpre0[0],
                                                                               op0=ALU.mult, op1=ALU.mult),
                  reads=[xt[1], msq[1], gpre0[1]], writes=[xb[1]])
            art = arr.next(); ait = air.next()
            for tt, dstt in ((0, art), (1, ait)):
                for half in range(2):
                    bk = bankr.next()
                    lo = (j * 2 + tt) * 128
                    mm(bk[0], bk[1], ga[0][:, lo:lo + 128], xb[0][:, half * 512:(half + 1) * 512], True, True, [ga[1], xb[1]])
                    evac(dstt[0][:, half * 512:(half + 1) * 512], dstt[1], bk[0], bk[1])
            for dstt, dd in ((art, ar_d), (ait, ai_d)):
                for e_ in range(2):
                    dst = dd[:, 2 * j + e_, :]
                    P.add("sp", lambda e, dstt=dstt, e_=e_, dst=dst: e.dma_start(out=dst, in_=dstt[0][e_ * 64:(e_ + 1) * 64, :]),
                          reads=[dstt[1]], writes=[a_bufs[j]], dma=True)

        P.barrier()
        A.off = persist_mark
        gpost0 = load_gain(g_mix_post, 0)
        gfpre0 = load_gain(g_ffn_pre, 0)
        gfpost0 = load_gain(g_ffn_post, 0)
        fc = A.tile(512, BF16)
        ch = A.tile(256, BF16)
        P.add("sp", lambda e: e.dma_start(out=fc[0], in_=fc_d), writes=[fc[1]], dma=True)
        P.add("sp", lambda e: e.dma_start(out=ch[0], in_=ch_d), writes=[ch[1]], dma=True)
        wcs = A.tile(2 * 8 * 1024, BF16)
        st = make_ffn_state()
        Hr = A.ring(2, 4 * D, F32)
        a_r = A.ring(4, D, BF16); a_i = A.ring(4, D, BF16)
        ut = A.ring(1, 8 * 256, BF16)
        wf_t = st["wd"][0]
        P.add("sp", lambda e: e.dma_start(out=wf_t[0][:, 0:8192], in_=wb["wf"]), reads=wdeps("wf", 0, 128), writes=[wf_t[1]],
              dma=True)
        for g_ in range(8):
            for cs in range(2):
                for half in range(2):
                    bk = st["misc"].next()
                    mm(bk[0], bk[1], ch[0][:, cs * 128:(cs + 1) * 128],
                       wf_t[0][:, g_ * 1024 + half * 512:g_ * 1024 + (half + 1) * 512], True, True, [ch[1], wf_t[1]])
                    lo = cs * 8192 + g_ * 1024 + half * 512
                    evac(wcs[0][:, lo:lo + 512], wcs[1], bk[0], bk[1])
        x_r = x.rearrange("(b a) c -> a b c", a=64)
        h1_r = h1.rearrange("(b a) c -> a b c", a=64)
        def loads_b(blk):
            H = Hr.next()
            for t in range(4):
                k1 = blk * 4 + t
                src = x_r[k1]
                P.add("sp", lambda e, H=H, t=t, src=src: e.dma_start(out=H[0][:, t * D:(t + 1) * D], in_=src), writes=[H[1]], dma=True)
            tiles = []
            for t in range(4):
                k1 = blk * 4 + t
                art = a_r.next(); ait = a_i.next()
                P.add("sp", lambda e, art=art, k1=k1: e.dma_start(out=art[0], in_=ar_d[k1]), reads=a_bufs, writes=[art[1]], dma=True)
                P.add("sp", lambda e, ait=ait, k1=k1: e.dma_start(out=ait[0], in_=ai_d[k1]), reads=a_bufs, writes=[ait[1]], dma=True)
                tiles.append((art, ait))
            return (H, tiles)

        def front_b(ctx):
            H, tiles = ctx
            for t in range(4):
                art, ait = tiles[t]
                utt = ut.next()
                for pair in range(4):
                    bk = st["misc"].next()
                    for q in range(2):
                        i = pair * 2 + q
                        o_ = bk[0][:, q * 256:(q + 1) * 256]
                        mm(o_, bk[1], art[0][:, i * 128:(i + 1) * 128], fc[0][:, 0:256], True, False, [art[1], fc[1]])
                        mm(o_, bk[1], ait[0][:, i * 128:(i + 1) * 128], fc[0][:, 256:512], False, True, [ait[1], fc[1]])
                    evac(utt[0][:, pair * 512:(pair + 1) * 512], utt[1], bk[0], bk[1])
                mh = []
                for half in range(2):
                    db = st["mbanks"].next()
                    n_ = 0
                    for i in range(8):
                        for cs in range(2):
                            mm(db[0], db[1], utt[0][:, i * 256 + cs * 128:i * 256 + (cs + 1) * 128],
                               wcs[0][:, cs * 8192 + i * 1024 + half * 512:cs * 8192 + i * 1024 + (half + 1) * 512],
                               n_ == 0, n_ == 15, [utt[1], wcs[1]])
                            n_ += 1
                    mh.append(db)
                post_norm_add(mh, gpost0, H[0][:, t * D:(t + 1) * D], H[1], st["msq"], st["junk"], st["tsb"])
            return H

        def store_b(H, blk):
            for t in range(4):
                k1 = blk * 4 + t
                dst = h1_r[k1]
                P.add("sp", lambda e, H=H, t=t, dst=dst: e.dma_start(out=dst, in_=H[0][:, t * D:(t + 1) * D]),
                      reads=[H[1]], writes=[h1_bufs[blk]], dma=True)

        pipelined_blocks(st, 0, loads_b, front_b, gfpre0, gfpost0, store_b)

    if do1:
        pump(10 ** 6)
        P.barrier()
        A.off = persist_mark
        gpre1 = load_gain(g_mix_pre, 1)
        wqk_t = A.tile(16 * 1024, BF16)
        wv_t = A.tile(8192, BF16)
        for m in range(16):
            P.add("sp", lambda e, m=m: e.dma_start(out=wqk_t[0][:, m * 1024:(m + 1) * 1024], in_=wb["wqk"][m * 128:(m + 1) * 128, :]),
                  reads=wdeps("wqk", m * 128, (m + 1) * 128), writes=[wqk_t[1]], dma=True)
        P.add("sp", lambda e: e.dma_start(out=wv_t[0], in_=wb["wv"]), reads=wdeps("wv", 0, 128), writes=[wv_t[1]], dma=True)
        Hr = A.ring(2, 4 * D, F32)
        xbr = A.ring(4, D, BF16)
        xnT = A.tile(8 * 512, BF16)
        msqr = A.ring(4, 8, F32)
        junk = A.tile(D, BF16)
        qko = A.ring(4, 512, BF16)
        vo = A.ring(2, D, BF16)
        misc = Ring([banks[6], banks[7]])
        pb = Ring(banks[0:6])
        h1_t = h1.rearrange("(n p) c -> n p c", p=128)
        v_w = v_d.rearrange("h k t e -> t k h e")
        all_h1 = h1_bufs if do0 else []
        def q_loads(blk):
            H = Hr.next()
            for t in range(4):
                P.add("sp", lambda e, H=H, t=t, blk=blk: e.dma_start(out=H[0][:, t * D:(t + 1) * D], in_=h1_t[blk * 4 + t]),
                      reads=all_h1, writes=[H[1]], dma=True)
            return H

        Hq = q_loads(0)
        xbs = norm_part(Hq, gpre1, xbr, msqr, junk)
        transpose_part(xbs, xnT, misc)
        for blk in range(16):
            if blk + 1 < 16:
                Hq = q_loads(blk + 1)
                xbs = norm_part(Hq, gpre1, xbr, msqr, junk)
            for m in range(16):
                bk = pb.next()
                for i in range(8):
                    mm(bk[0], bk[1], wqk_t[0][:, m * 1024 + i * 128:m * 1024 + (i + 1) * 128], xnT[0][:, i * 512:(i + 1) * 512],
                       i == 0, i == 7, [wqk_t[1], xnT[1]])
                qk = qko.next()
                evac(qk[0], qk[1], bk[0], bk[1])
                dst = (qT_d if m < 8 else kT_d)[m % 8, :, blk * 512:(blk + 1) * 512]
                P.add("sp", lambda e, qk=qk, dst=dst: e.dma_start(out=dst, in_=qk[0]), reads=[qk[1]], writes=[qkv_bufs[blk]], dma=True)
            for t in range(4):
                vt = vo.next()
                for half in range(2):
                    bk = pb.next()
                    for i in range(8):
                        mm(bk[0], bk[1], xnT[0][:, i * 512 + t * 128:i * 512 + (t + 1) * 128],
                           wv_t[0][:, i * 1024 + half * 512:i * 1024 + (half + 1) * 512], i == 0, i == 7, [xnT[1], wv_t[1]])
                    evac(vt[0][:, half * 512:(half + 1) * 512], vt[1], bk[0], bk[1])
                dst = v_w[blk * 4 + t]
                P.add("sp", lambda e, vt=vt, dst=dst: e.dma_start(out=dst, in_=vt[0].rearrange("p (h e) -> p h e", e=128)),
                      reads=[vt[1]], writes=[qkv_bufs[blk]], dma=True)
            if blk + 1 < 16:
                transpose_part(xbs, xnT, misc)

        P.barrier()
        A.off = persist_mark
        q1 = A.tile(512, F32); t1 = A.tile(2048, F32); bct = A.tile(NH * 128, F32)
        P.add("sp", lambda e: e.dma_start(out=q1[0], in_=q1_d), writes=[q1[1]], dma=True)
        P.add("sp", lambda e: e.dma_start(out=t1[0], in_=t1_d), writes=[t1[1]], dma=True)
        P.add("sp", lambda e: e.dma_start(out=bct[0], in_=bc_d), writes=[bct[1]], dma=True)
        lamt = A.tile(4 * 64, F32); lamw = A.tile(2 * 64, F32); lams = A.tile(8, F32)
        for n_, ap_ in enumerate(lam_in):
            P.add("sp", lambda e, n_=n_, ap_=ap_: e.dma_start(out=lamt[0][:, n_ * 64:(n_ + 1) * 64], in_=ap_.partition_broadcast(128)),
                  writes=[lamt[1]], dma=True)
        for n_ in range(2):
            P.add("dve", lambda e, n_=n_: e.tensor_tensor(out=lamw[0][:, n_ * 64:(n_ + 1) * 64], in0=lamt[0][:, n_ * 128:n_ * 128 + 64],
                                                          in1=lamt[0][:, n_ * 128 + 64:n_ * 128 + 128], op=ALU.mult),
                  reads=[lamt[1]], writes=[lamw[1]])
            P.add("act", lambda e, n_=n_: e.activation(out=lamw[0][:, n_ * 64:(n_ + 1) * 64], in_=lamw[0][:, n_ * 64:(n_ + 1) * 64],
                                                       func=AF.Copy, accum_out=lams[0][:, n_:n_ + 1]),
                  reads=[lamw[1]], writes=[lamw[1], lams[1]])
        P.add("act", lambda e: e.activation(out=lams[0][:, 2:4], in_=lams[0][:, 0:2], func=AF.Exp), reads=[lams[1]], writes=[lams[1]])
        P.add("dve", lambda e: e.tensor_tensor(out=lams[0][:, 4:5], in0=lams[0][:, 3:4], in1=lams[0][:, 2:3], op=ALU.subtract),
              reads=[lams[1]], writes=[lams[1]])
        P.add("dve", lambda e: e.tensor_scalar(out=lams[0][:, 5:6], in0=lams[0][:, 4:5], scalar1=-LAM_INIT, scalar2=None, op0=ALU.add),
              reads=[lams[1]], writes=[lams[1]])
        neg_lam = lams[0][:, 5:6]
        gsub = A.tile(128, F32)
        P.add("sp", lambda e: e.dma_start(out=gsub[0], in_=subln.partition_broadcast(128)), writes=[gsub[1]], dma=True)
        P.add("dve", lambda e: e.tensor_scalar(out=gsub[0], in0=gsub[0], scalar1=1.0 - LAM_INIT, scalar2=None, op0=ALU.mult),
              reads=[gsub[1]], writes=[gsub[1]])
        qT_r = Ring([(A.tile(S, BF16), A.tile(S, BF16)) for _ in range(2)])
        for qp0, qp1 in qT_r.items:
            P.add("pool", lambda e, qp0=qp0: e.memset(qp0[0][64:128, :], 0.0), writes=[qp0[1]])
            P.add("pool", lambda e, qp1=qp1: e.memset(qp1[0][0:64, :], 0.0), writes=[qp1[1]])
        kT_r = A.ring(2, S, BF16); v_r = A.ring(2, 64 * VP, BF16)
        osb_r = A.ring(2, 8 * 132, F32)
        for it in v_r.items:
            v3 = it[0].rearrange("p (t e) -> p t e", e=VP)
            P.add("pool", lambda e, v3=v3: e.memset(v3[:, :, 128:129], 1.0), writes=[it[1]])
        tmp_r = A.ring(5, 512, F32)
        E_r = A.ring(6, 512, BF16)
        ep = A.ring(2, 16, F32)
        o1_r = A.ring(2, 512, F32); o_r = A.ring(2, 512, F32)
        msqr = A.ring(4, 8, F32)
        junk = A.tile(128, BF16)
        ot_r = A.ring(2, 512, BF16)
        st_banks = Ring(banks[0:5])
        o_dv = o_d.rearrange("(b j p) (h e) -> b h p j e", j=4, p=128, e=128)
        deferred = []
        for h in range(NH):
            sl = SLOPES[h]
            qP = qT_r.next(); kT = kT_r.next(); vv = v_r.next()
            for c_ in range(2):
                P.add("sp", lambda e, qP=qP, h=h, c_=c_: e.dma_start(out=qP[c_][0][c_ * 64:(c_ + 1) * 64, :],
                                                                   in_=qT_d[h, c_ * 64:(c_ + 1) * 64, :]),
                      reads=qkv_bufs, writes=[qP[c_][1]], dma=True)
            P.add("sp", lambda e, kT=kT, h=h: e.dma_start(out=kT[0], in_=kT_d[h]), reads=qkv_bufs, writes=[kT[1]], dma=True)
            v3 = vv[0].rearrange("p (t e) -> p t e", e=VP)
            for q4 in range(4):
                P.add("sp", lambda e, v3=v3, h=h, q4=q4: e.dma_start(out=v3[:, q4 * 16:(q4 + 1) * 16, 0:128],
                                                                     in_=v_d[h, :, q4 * 16:(q4 + 1) * 16, :]),
                      reads=qkv_bufs, writes=[vv[1]], dma=True)
            units = []
            for qb in range(16):
                pl = []
                for kt in range(64):
                    n = kt - 4 * qb
                    if 0 <= n <= 3:
                        kind = ("D", n)
                    elif n >= 4:
                        if sl * (128 * n - 511) > ATT_THRESH:
                            continue
                        kind = ("R", n)
                    else:
                        if sl * (128 * (-n) - 127) > ATT_THRESH:
                            continue
                        kind = ("L", -n)
                    for comp in range(2):
                        pl.append([qb, kt, comp, kind, False, False])
                pl[0][4] = True
                pl[-1][5] = True
                units.extend(pl)
            nu = len(units)
            stb = {}

            def issue_st(i):
                qb_, kt_, comp_ = units[i][0:3]
                bk = st_banks.next()
                stb[i] = bk
                mm(bk[0], bk[1], kT[0][:, kt_ * 128:(kt_ + 1) * 128],
                   qP[comp_][0][:, qb_ * 512:(qb_ + 1) * 512], True, True, [kT[1], qP[comp_][1]])

            for i in range(min(LOOKAHEAD, nu)):
                issue_st(i)
            first_in_bank = {}
            for i in range(nu):
                if i + LOOKAHEAD < nu:
                    issue_st(i + LOOKAHEAD)
                qb, kt, comp, kind, is_first, is_last = units[i]
                if is_first:
                    first_in_bank = {}
                bk = stb.pop(i)
                tmp = tmp_r.next()
                if kind[0] == "D":
                    in1 = t1[0][:, kind[1] * 512:(kind[1] + 1) * 512]; in1b = t1[1]; op1 = ALU.add
                    bias = 0.0; br = []
                elif kind[0] == "R":
                    in1 = q1[0]; in1b = q1[1]; op1 = ALU.add
                    c_ = h * 128 + kind[1]
                    bias = bct[0][:, c_:c_ + 1]; br = [bct[1]]
                else:
                    in1 = q1[0]; in1b = q1[1]; op1 = ALU.subtract
                    c_ = h * 128 + 64 + kind[1]
                    bias = bct[0][:, c_:c_ + 1]; br = [bct[1]]
                P.add("dve", lambda e, tmp=tmp, bk=bk, in1=in1, op1=op1, sl=sl: e.scalar_tensor_tensor(
                    out=tmp[0], in0=bk[0], scalar=0.125 / sl, in1=in1, op0=ALU.mult, op1=op1),
                    reads=[bk[1], in1b], writes=[tmp[1]])
                E = E_r.next()
                P.add("act", lambda e, E=E, tmp=tmp, bias=bias, sl=sl: e.activation(out=E[0], in_=tmp[0], func=AF.Exp, bias=bias, scale=sl),
                      reads=[tmp[1]] + br, writes=[E[1]])
                for jq in range(4):
                    g_ = comp * 4 + jq
                    bkey = 5 + g_ // 3
                    ob = banks[bkey]
                    oa = ob[0][:, (g_ % 3) * 160:(g_ % 3) * 160 + 129]
                    start = bkey not in first_in_bank
                    first_in_bank[bkey] = True
                    mm(oa, ob[1], E[0][:, jq * 128:(jq + 1) * 128], v3[:, kt, 0:129], start, is_last or (i + 1 < nu and units[i + 1][5]),
                       [E[1], vv[1]], skip=True)
                if deferred and not is_last:
                    deferred.pop(0)()
                if not is_last:
                    continue
                osb = osb_r.next()
                osb4 = osb[0].rearrange("p (a c) -> p a c", c=132)
                for bi in range(3):
                    ng = 3 if bi < 2 else 2
                    evac(osb4[:, 3 * bi:3 * bi + ng, 0:129], osb[1],
                         banks[5 + bi][0][:, 0:480].rearrange("p (g c) -> p g c", c=160)[:, 0:ng, 0:129], banks[5 + bi][1])
                ott = ot_r.next()
                epp = ep.next()
                o1 = o1_r.next(); oo = o_r.next(); msq = msqr.next()

                def s1(osb=osb, osb4=osb4, epp=epp):
                    e3 = epp[0][:, 0:12].rearrange("p (a o) -> p a o", o=1)
                    P.add("dve", lambda e: e.reciprocal(out=e3[:, 0:4, :], in_=osb4[:, 4:8, 128:129]), reads=[osb[1]], writes=[epp[1]])
                    P.add("dve", lambda e: e.tensor_tensor(out=e3[:, 4:8, :], in0=e3[:, 0:4, :], in1=osb4[:, 0:4, 128:129], op=ALU.mult),
                          reads=[osb[1], epp[1]], writes=[epp[1]])
                    P.add("dve", lambda e: e.tensor_scalar(out=epp[0][:, 8:12], in0=epp[0][:, 4:8], scalar1=neg_lam, scalar2=None,
                                                           op0=ALU.mult),
                          reads=[epp[1], lams[1]], writes=[epp[1]])

                def s2(osb=osb, epp=epp, oo=oo):
                    for jq in range(4):
                        P.add("dve", lambda e, jq=jq: e.scalar_tensor_tensor(
                            out=oo[0][:, jq * 128:(jq + 1) * 128], in0=osb[0][:, (4 + jq) * 132:(4 + jq) * 132 + 128],
                            scalar=epp[0][:, 8 + jq:9 + jq], in1=osb[0][:, jq * 132:jq * 132 + 128], op0=ALU.mult, op1=ALU.add),
                            reads=[osb[1], epp[1]], writes=[oo[1]])

                def s3(oo=oo, msq=msq):
                    for jq in range(4):
                        P.add("act", lambda e, jq=jq: e.activation(out=junk[0], in_=oo[0][:, jq * 128:(jq + 1) * 128], func=AF.Square,
                                                                   scale=128.0 ** -0.5, accum_out=msq[0][:, jq:jq + 1]),
                              reads=[oo[1]], writes=[junk[1], msq[1]])

                def s4(msq=msq, osb=osb, osb4=osb4):
                    m3 = msq[0][:, 0:8].rearrange("p (a o) -> p a o", o=1)
                    P.add("pool", lambda e: e.tensor_tensor(out=m3[:, 4:8, :], in0=osb4[:, 0:4, 128:129], in1=osb4[:, 0:4, 128:129], op=ALU.mult),
                          reads=[osb[1]], writes=[msq[1]])
                    P.add("pool", lambda e: e.tensor_tensor(out=msq[0][:, 4:8], in0=msq[0][:, 4:8], in1=cst[0][:, 0:4], op=ALU.mult),
                          reads=[msq[1], cst[1]], writes=[msq[1]])
                    P.add("pool", lambda e: e.tensor_tensor(out=msq[0][:, 4:8], in0=msq[0][:, 4:8], in1=msq[0][:, 0:4], op=ALU.add),
                          reads=[msq[1]], writes=[msq[1]])
                    P.add("pool", lambda e: e.tensor_tensor(out=msq[0][:, 4:8], in0=msq[0][:, 4:8], in1=cst[0][:, 4:8], op=ALU.pow),
                          reads=[msq[1], cst[1]], writes=[msq[1]])

                def s5():
                    pass

                def s6(oo=oo, msq=msq, ott=ott, qb=qb, h=h):
                    for jq in range(4):
                        P.add("dve", lambda e, jq=jq: e.scalar_tensor_tensor(
                            out=ott[0][:, jq * 128:(jq + 1) * 128], in0=oo[0][:, jq * 128:(jq + 1) * 128], scalar=msq[0][:, 4 + jq:5 + jq],
                            in1=gsub[0], op0=ALU.mult, op1=ALU.mult),
                            reads=[oo[1], msq[1], gsub[1]], writes=[ott[1]])
                    dst = o_dv[qb, h]
                    P.add("sp", lambda e: e.dma_start(out=dst, in_=ott[0].rearrange("p (j e) -> p j e", e=128)),
                          reads=[ott[1]], writes=[o_bufs[qb]], dma=True)

                while deferred:
                    deferred.pop(0)()
                deferred.extend([s1, s2, s3, s4, s5, s6])
        while deferred:
            deferred.pop(0)()

        P.barrier()
        A.off = persist_mark
        gpost1 = load_gain(g_mix_post, 1)
        gfpre1 = load_gain(g_ffn_pre, 1)
        gfpost1 = load_gain(g_ffn_post, 1)
        wo_t = A.tile(8192, BF16)
        P.add("sp", lambda e: e.dma_start(out=wo_t[0], in_=wb["wo"]), reads=wdeps("wo", 0, 128), writes=[wo_t[1]], dma=True)
        st = make_ffn_state()
        Hr = A.ring(2, 4 * D, F32)
        ob_r = A.ring(4, D, BF16)
        oT = A.tile(8 * 512, BF16)
        o_t = o_d.rearrange("(n p) c -> n p c", p=128)
        out_t = out.rearrange("(n p) c -> n p c", p=128)
        all_h1 = h1_bufs if do0 else []
        def loads_u(blk):
            H = Hr.next()
            for t in range(4):
                P.add("sp", lambda e, H=H, t=t, blk=blk: e.dma_start(out=H[0][:, t * D:(t + 1) * D], in_=h1_t[blk * 4 + t]),
                      reads=all_h1, writes=[H[1]], dma=True)
            tiles = []
            for t in range(4):
                obt = ob_r.next()
                P.add("sp", lambda e, obt=obt, t=t, blk=blk: e.dma_start(out=obt[0], in_=o_t[blk * 4 + t]), reads=o_bufs,
                      writes=[obt[1]], dma=True)
                tiles.append(obt)
            return (H, tiles)

        def front_u(ctx):
            H, tiles = ctx
            for t in range(4):
                obt = tiles[t]
                for half in range(2):
                    bk = st["misc"].next()
                    for q in range(4):
                        i = half * 4 + q
                        mm(bk[0][:, q * 128:(q + 1) * 128], bk[1], obt[0][:, i * 128:(i + 1) * 128], ident[0], True, True,
                           [obt[1], ident[1]])
                    dst = oT[0].rearrange("p (i k) -> p i k", k=512)[:, half * 4:half * 4 + 4, t * 128:(t + 1) * 128]
                    evac(dst, oT[1], bk[0].rearrange("p (q k) -> p q k", k=128), bk[1])
            for t in range(4):
                mh = []
                for half in range(2):
                    db = st["mbanks"].next()
                    for i in range(8):
                        mm(db[0], db[1], oT[0][:, i * 512 + t * 128:i * 512 + (t + 1) * 128],
                           wo_t[0][:, i * 1024 + half * 512:i * 1024 + (half + 1) * 512], i == 0, i == 7, [oT[1], wo_t[1]])
                    mh.append(db)
                post_norm_add(mh, gpost1, H[0][:, t * D:(t + 1) * D], H[1], st["msq"], st["junk"], st["tsb"])
            return H

        def store_u(H, blk):
            for t in range(4):
                P.add("sp", lambda e, H=H, t=t, blk=blk: e.dma_start(out=out_t[blk * 4 + t], in_=H[0][:, t * D:(t + 1) * D]),
                      reads=[H[1]], dma=True)

        pipelined_blocks(st, 1, loads_u, front_u, gfpre1, gfpost1, store_u)

    P.emit()
    return nc


_PROG_CACHE = {}


def _common_inputs(inp):
    global _CONSTS
    if _CONSTS is None:
        _CONSTS = _host_consts()
    m = dict(_CONSTS)
    m.update(_layout_weights(inp))
    for k in ("norm_mix_pre", "norm_mix_post", "norm_ffn_pre", "norm_ffn_post", "diff_lambda_q1", "diff_lambda_k1",
              "diff_lambda_q2", "diff_lambda_k2", "diff_subln_g"):
        m[k] = np.ascontiguousarray(np.asarray(inp[k], dtype=np.float32))
    return m


L0_KEYS = ("norm_mix_pre", "norm_mix_post", "norm_ffn_pre", "norm_ffn_post", "diff_lambda_q1", "diff_lambda_k1",
           "diff_lambda_q2", "diff_lambda_k2", "diff_subln_g", "ident", "ga", "fc", "ch", "q1", "t1", "bc",
           "wg", "wu", "wd", "wqk", "wv", "wo", "wf")

FUSED = True


def kernel(**inputs):
    x = np.ascontiguousarray(np.asarray(inputs["x"], dtype=np.float32))
    common = _common_inputs(inputs)
    if FUSED:
        nc = build("full")
        in_maps = [dict(common, x=x[b]) for b in range(8)]
        res = run_bass_kernel_spmd(nc, in_maps, core_ids=list(range(8)))
        return np.stack([r["out"] for r in res.results], axis=0)
    nc0 = build("L0")
    res0 = run_bass_kernel_spmd(nc0, [dict(common, x=x[b]) for b in range(8)], core_ids=list(range(8)))
    h1 = [r["h1"] for r in res0.results]
    nc1 = build("L1")
    res1 = run_bass_kernel_spmd(nc1, [dict(common, h1=h1[b]) for b in range(8)], core_ids=list(range(8)))
    return np.stack([r["out"] for r in res1.results], axis=0)
```

```python
import math
import numpy as np
import ml_dtypes
import concourse.bass as bass
import concourse.mybir as mybir
from concourse.bass_utils import run_bass_kernel_spmd

F32 = mybir.dt.float32
BF16 = mybir.dt.bfloat16
ALU = mybir.AluOpType
AF = mybir.ActivationFunctionType

S = 8192
D = 1024
FF = 2816
NJ = FF // 128
NH = 8
EPS = 1e-6
LAM_INIT = 0.8 - 0.6 * math.exp(-0.3 * 1)
SLOPES = [2.0 ** (-(h + 1)) for h in range(NH)]
VP = 144
ATT_THRESH = 48.0

SAME_ENGINE_SYNC = True
DMA_RING = 12
SEM_EPOCH = 16000
LOOKAHEAD = 4
STREAMS = ("pe", "act", "dve", "pool", "sp")


class Buf:
    __slots__ = ("lastw", "lastr")

    def __init__(self):
        self.lastw = {}
        self.lastr = {}


class Op:
    __slots__ = ("stream", "fn", "deps", "sig", "key", "key2", "val", "dma")


class Prog:
    def __init__(self, nc):
        self.nc = nc
        self.ops = {s: [] for s in STREAMS}
        self.dma_n = {s: 0 for s in STREAMS}
        self.ring_last = {}
        self.pending = {s: set() for s in STREAMS}
        self.last = {s: None for s in STREAMS}

    def add(self, stream, fn, reads=(), writes=(), dma=False):
        op = Op()
        op.stream, op.fn, op.dma, op.sig, op.val = stream, fn, dma, dma, None
        deps = set()
        for b in reads:
            deps.update(b.lastw.values())
        for b in writes:
            deps.update(b.lastw.values())
            deps.update(b.lastr.values())
        if dma:
            idx = self.dma_n[stream] % DMA_RING
            self.dma_n[stream] += 1
            op.key = (stream, idx)
            prev = self.ring_last.get(op.key)
            if prev is not None:
                deps.add(prev)
            self.ring_last[op.key] = op
        else:
            op.key = stream
        if self.pending[stream]:
            deps.update(self.pending[stream])
            self.pending[stream] = set()
        fdeps = []
        for d in deps:
            if (not d.dma) and (not dma) and d.stream == stream:
                if stream == "pe" or not SAME_ENGINE_SYNC:
                    continue
            d.sig = True
            fdeps.append(d)
        op.deps = fdeps
        for b in reads:
            b.lastr[op.key] = op
        for b in writes:
            b.lastw = {op.key: op}
            b.lastr = {}
        self.ops[stream].append(op)
        if not dma:
            self.last[stream] = op
        return op

    def barrier(self):
        allops = set(o for o in self.last.values() if o is not None)
        allops.update(self.ring_last.values())
        for s in STREAMS:
            self.pending[s] = set(allops)
        for o in allops:
            o.sig = True

    def emit(self):
        nc = self.nc
        sems = {}
        cnt = {}
        epoch = {}

        def sem(key):
            if key not in sems:
                sems[key] = nc.alloc_semaphore("s_" + "_".join(str(k) for k in key))
            return sems[key]

        for s_ in STREAMS:
            for op in self.ops[s_]:
                if op.dma:
                    op.key2 = ("d",) + op.key
                    cnt[op.key2] = cnt.get(op.key2, 0) + 16
                    op.val = cnt[op.key2]
                    sem(op.key2)
                elif op.sig:
                    ep_ = epoch.get(s_, 0)
                    k2 = ("c", s_, ep_)
                    if cnt.get(k2, 0) >= SEM_EPOCH:
                        ep_ += 1
                        epoch[s_] = ep_
                        k2 = ("c", s_, ep_)
                    op.key2 = k2
                    cnt[k2] = cnt.get(k2, 0) + 1
                    op.val = cnt[k2]
                    sem(k2)

        def replay(stream):
            def body(eng):
                waited = {}
                for op in self.ops[stream]:
                    for d in op.deps:
                        if waited.get(d.key2, 0) >= d.val:
                            continue
                        eng.wait_ge(sems[d.key2], d.val)
                        waited[d.key2] = d.val
                    inst = op.fn(eng)
                    if op.sig:
                        inst.then_inc(sems[op.key2], 16 if op.dma else 1)
                for key, v in cnt.items():
                    if key[0] == "d" and key[1] == stream and waited.get(key, 0) < v:
                        eng.wait_ge(sems[key], v)
            return body

        with nc.Block() as block:
            block.tensor(replay("pe"))
            block.scalar(replay("act"))
            block.vector(replay("dve"))
            block.gpsimd(replay("pool"))
            block.sync(replay("sp"))


class Arena:
    def __init__(self, nc, nbytes):
        self.ap = nc.alloc_sbuf_tensor("arena", [128, nbytes // 2], BF16).ap()
        self.cap = nbytes // 2
        self.off = 0

    def alloc(self, n, dt):
        k = n * 2 if dt == F32 else n
        k = (k + 15) // 16 * 16
        assert self.off + k <= self.cap, ("SBUF arena overflow", self.off, k, self.cap)
        a = self.ap[:, self.off:self.off + n * (2 if dt == F32 else 1)]
        self.off += k
        return a.bitcast(F32) if dt == F32 else a

    def tile(self, n, dt):
        return (self.alloc(n, dt), Buf())

    def ring(self, cnt, n, dt):
        return Ring([self.tile(n, dt) for _ in range(cnt)])


class Ring:
    def __init__(self, items):
        self.items = items
        self.i = 0

    def next(self):
        it = self.items[self.i % len(self.items)]
        self.i += 1
        return it


def _host_consts():
    bf = ml_dtypes.bfloat16
    c = {}
    c["ident"] = np.eye(128, dtype=np.float32).astype(bf)
    s1 = np.arange(64)
    k1 = np.arange(64)
    GA = np.zeros((128, 64, 2, 128), np.float32)
    for j in range(64):
        for e in range(2):
            s = 128 * s1 + 2 * j + e
            ang = 2 * np.pi * ((s[:, None] * k1[None, :]) % S) / S
            GA[e * 64:(e + 1) * 64, j, 0, e * 64:(e + 1) * 64] = np.cos(ang) / 32
            GA[e * 64:(e + 1) * 64, j, 1, e * 64:(e + 1) * 64] = -np.sin(ang) / 32
    c["ga"] = GA.reshape(128, 64 * 2 * 128).astype(bf)
    a = np.arange(128)
    ang = 2 * np.pi * ((a[:, None] * a[None, :]) % 128) / 128
    C, Sn = np.cos(ang), np.sin(ang)
    c["fc"] = np.concatenate([C / 32, -Sn / 32, Sn / 32, C / 32], axis=1).astype(bf)
    c["ch"] = np.concatenate([C, Sn], axis=1).astype(bf)
    qq = np.arange(512, dtype=np.float32)
    c["q1"] = np.broadcast_to(qq[None, :], (128, 512)).astype(np.float32).copy()
    kk = np.arange(128, dtype=np.float32)
    T1 = np.zeros((128, 4, 512), np.float32)
    for j in range(4):
        T1[:, j, :] = -np.abs(128 * j + kk[:, None] - qq[None, :])
    c["t1"] = T1.reshape(128, 2048)
    bc = np.zeros((128, NH, 2, 64), np.float32)
    n = np.arange(64, dtype=np.float32)
    for h in range(NH):
        sl = SLOPES[h]
        bc[:, h, 0, :] = -sl * kk[:, None] - sl * 128 * n[None, :]
        bc[:, h, 1, :] = sl * kk[:, None] - sl * 128 * n[None, :]
    c["bc"] = bc.reshape(128, NH * 2 * 64)
    return c


_CONSTS = None


def _layout_weights(inp):
    w = {}
    f = lambda a: np.ascontiguousarray(a, dtype=np.float32)
    wg = np.asarray(inp["ffn_w_gate"]); wu = np.asarray(inp["ffn_w_up"]); wd = np.asarray(inp["ffn_w_down"])
    w["wg"] = f(wg.reshape(2, 8, 128, NJ, 128).transpose(0, 3, 2, 1, 4)).reshape(2 * NJ * 128, 1024)
    w["wu"] = f(wu.reshape(2, 8, 128, NJ, 128).transpose(0, 3, 2, 1, 4)).reshape(2 * NJ * 128, 1024)
    w["wd"] = f(wd.reshape(2, NJ, 128, 2, 512).transpose(0, 3, 2, 1, 4)).reshape(2 * 2 * 128, NJ * 512)
    wqkv = np.asarray(inp["diff_w_qkv"])[0]
    w["wqk"] = f(wqkv[:, :2048].reshape(8, 128, 16, 128).transpose(2, 1, 0, 3)).reshape(16 * 128, 1024)
    w["wv"] = f(wqkv[:, 2048:].reshape(8, 128, 1024).transpose(1, 0, 2)).reshape(128, 8192)
    w["wo"] = f(np.asarray(inp["diff_w_o"])[0].reshape(8, 128, 1024).transpose(1, 0, 2)).reshape(128, 8192)
    w["wf"] = f(np.asarray(inp["fourier_w_o"])[0].reshape(8, 128, 1024).transpose(1, 0, 2)).reshape(128, 8192)
    return w


def build(mode="full"):
    nc = bass.Bass("TRN2", target_bir_lowering=False)
    P = Prog(nc)
    do0 = mode in ("full", "L0")
    do1 = mode in ("full", "L1")

    def din(name, shape, dt=F32):
        return nc.dram_tensor(name, list(shape), dt, kind="ExternalInput").ap()

    def dscr(name, shape, dt):
        return nc.dram_tensor(name, list(shape), dt).ap()

    x = din("x", [S, D]) if do0 else None
    if mode == "L0":
        h1 = nc.dram_tensor("h1", [S, D], F32, kind="ExternalOutput").ap()
    elif mode == "L1":
        h1 = din("h1", [S, D])
    else:
        h1 = dscr("h1", [S, D], F32)
    out = nc.dram_tensor("out", [S, D], F32, kind="ExternalOutput").ap() if do1 else None
    g_mix_pre = din("norm_mix_pre", [2, D]); g_mix_post = din("norm_mix_post", [2, D])
    g_ffn_pre = din("norm_ffn_pre", [2, D]); g_ffn_post = din("norm_ffn_post", [2, D])
    lam_in = [din(n, [1, 64]) for n in ("diff_lambda_q1", "diff_lambda_k1", "diff_lambda_q2", "diff_lambda_k2")]
    subln = din("diff_subln_g", [1, 128])
    ident_d = din("ident", [128, 128], BF16)
    ga_d = din("ga", [128, 64 * 2 * 128], BF16)
    fc_d = din("fc", [128, 512], BF16)
    ch_d = din("ch", [128, 256], BF16)
    q1_d = din("q1", [128, 512]); t1_d = din("t1", [128, 2048]); bc_d = din("bc", [128, NH * 2 * 64])
    wsrc = {"wg": din("wg", [2 * NJ * 128, 1024]), "wu": din("wu", [2 * NJ * 128, 1024]),
            "wd": din("wd", [2 * 2 * 128, NJ * 512]), "wqk": din("wqk", [16 * 128, 1024]),
            "wv": din("wv", [128, 8192]), "wo": din("wo", [128, 8192]), "wf": din("wf", [128, 8192])}
    wb = {k: dscr(k + "_b", v.shape, BF16) for k, v in wsrc.items()}
    wbuf = {}

    cast_q = []

    def cast(key, r0, r1, c0, c1):
        b = Buf()
        wbuf.setdefault(key, []).append(((r0, r1, c0, c1), b))
        src = wsrc[key][r0:r1, c0:c1]; dst = wb[key][r0:r1, c0:c1]
        cast_q.append((src, dst, b))

    def pump(n):
        for _ in range(min(n, len(cast_q))):
            src, dst, b = cast_q.pop(0)
            P.add("pool", lambda e, src=src, dst=dst: e.dma_start(out=dst, in_=src), writes=[b], dma=True)

    def wdeps(key, r0, r1, c0=None, c1=None):
        res = []
        for (a0, a1, b0, b1), b in wbuf[key]:
            if a0 < r1 and r0 < a1 and (c0 is None or (b0 < c1 and c0 < b1)):
                res.append(b)
        return res

    def cast_ffn(l):
        for j in range(NJ):
            r = (l * NJ + j) * 128
            cast("wg", r, r + 128, 0, 1024)
            cast("wu", r, r + 128, 0, 1024)
        for e in range(2):
            r = (l * 2 + e) * 128
            for q in range(4):
                cast("wd", r, r + 128, q * 2816, (q + 1) * 2816)

    if do0:
        for q in range(4):
            cast("wf", 0, 128, q * 2048, (q + 1) * 2048)
        cast_ffn(0)
    n_l0_casts = len(cast_q)
    if do1:
        for m in range(16):
            cast("wqk", m * 128, (m + 1) * 128, 0, 1024)
        for q in range(4):
            cast("wv", 0, 128, q * 2048, (q + 1) * 2048)
        for q in range(4):
            cast("wo", 0, 128, q * 2048, (q + 1) * 2048)
        cast_ffn(1)
    if not do0:
        pump(10 ** 6)
    else:
        pump(8)

    if do0:
        ar_d = dscr("ar_s", [64, 128, D], BF16); ai_d = dscr("ai_s", [64, 128, D], BF16)
        a_bufs = [Buf() for _ in range(64)]
    if do1:
        qT_d = dscr("qT_s", [NH, 128, S], BF16); kT_d = dscr("kT_s", [NH, 128, S], BF16)
        v_d = dscr("v_s", [NH, 128, 64, 128], BF16)
        o_d = dscr("o_s", [S, D], BF16)
        qkv_bufs = [Buf() for _ in range(16)]
        o_bufs = [Buf() for _ in range(16)]
    h1_bufs = [Buf() for _ in range(16)]

    A = Arena(nc, 207 * 1024)
    psum_all = nc.alloc_psum_tensor("psum_all", [128, 4096], F32).ap()
    banks = [(psum_all[:, i * 512:(i + 1) * 512], Buf()) for i in range(8)]
    ident = A.tile(128, BF16)
    cst = A.tile(8, F32)
    P.add("sp", lambda e: e.dma_start(out=ident[0], in_=ident_d), writes=[ident[1]], dma=True)
    P.add("pool", lambda e: e.memset(cst[0][:, 0:4], EPS), writes=[cst[1]])
    P.add("pool", lambda e: e.memset(cst[0][:, 4:8], -0.5), writes=[cst[1]])
    persist_mark = A.off

    def load_gain(g_d, i):
        t = A.tile(D, F32)
        src = g_d[i:i + 1, :].partition_broadcast(128)
        P.add("sp", lambda e: e.dma_start(out=t[0], in_=src), writes=[t[1]], dma=True)
        return t

    def rstd_from(msq, ncols, width_tag=None):
        ap, b = msq
        col = 0
        if ncols == 2:
            P.add("pool", lambda e: e.tensor_tensor(out=ap[:, 2:3], in0=ap[:, 0:1], in1=ap[:, 1:2], op=ALU.add),
                  reads=[b], writes=[b])
            col = 2
        P.add("pool", lambda e: e.tensor_tensor(out=ap[:, 3:4], in0=ap[:, col:col + 1], in1=cst[0][:, 0:1], op=ALU.add),
              reads=[b, cst[1]], writes=[b])
        P.add("pool", lambda e: e.tensor_tensor(out=ap[:, 4:5], in0=ap[:, 3:4], in1=cst[0][:, 4:5], op=ALU.pow),
              reads=[b, cst[1]], writes=[b])
        return ap[:, 4:5]

    def rstd4(msq):
        ap, b = msq
        P.add("pool", lambda e: e.tensor_tensor(out=ap[:, 4:8], in0=ap[:, 0:4], in1=cst[0][:, 0:4], op=ALU.add),
              reads=[b, cst[1]], writes=[b])
        P.add("pool", lambda e: e.tensor_tensor(out=ap[:, 4:8], in0=ap[:, 4:8], in1=cst[0][:, 4:8], op=ALU.pow),
              reads=[b, cst[1]], writes=[b])

    evac_rr = [0]

    def evac(dst, dstb, src, srcb, extra_reads=()):
        evac_rr[0] += 1
        if evac_rr[0] % 2:
            P.add("act", lambda e: e.activation(out=dst, in_=src, func=AF.Copy), reads=[srcb] + list(extra_reads), writes=[dstb])
        else:
            P.add("dve", lambda e: e.tensor_copy(out=dst, in_=src), reads=[srcb] + list(extra_reads), writes=[dstb])

    def mm(out, outb, lhsT, rhs, start, stop, reads, skip=False):
        if skip:
            P.add("pe", lambda e: e.matmul(out, lhsT=lhsT, rhs=rhs, start=start, stop=stop, skip_group_check=True),
                  reads=reads, writes=[outb])
        else:
            P.add("pe", lambda e: e.matmul(out, lhsT=lhsT, rhs=rhs, start=start, stop=stop), reads=reads, writes=[outb])

    def norm_part(H, gain, xb_ring, msq_ring, junk):
        Hap, Hb = H
        msq = msq_ring.next()
        for t in range(4):
            hs = Hap[:, t * D:(t + 1) * D]
            P.add("act", lambda e, hs=hs, t=t: e.activation(out=junk[0], in_=hs, func=AF.Square, scale=1.0 / 32,
                                                             accum_out=msq[0][:, t:t + 1]),
                  reads=[Hb], writes=[junk[1], msq[1]])
        rstd4(msq)
        xbs = []
        for t in range(4):
            hs = Hap[:, t * D:(t + 1) * D]
            xb = xb_ring.next()
            xbs.append(xb)
            P.add("dve", lambda e, hs=hs, xb=xb, t=t: e.scalar_tensor_tensor(out=xb[0], in0=hs, scalar=msq[0][:, 4 + t:5 + t], in1=gain[0],
                                                                            op0=ALU.mult, op1=ALU.mult),
                  reads=[Hb, msq[1], gain[1]], writes=[xb[1]])
        return xbs

    def transpose_part(xbs, xnT, misc_banks):
        for t in range(4):
            xb = xbs[t]
            for half in range(2):
                bk = misc_banks.next()
                for q in range(4):
                    i = half * 4 + q
                    mm(bk[0][:, q * 128:(q + 1) * 128], bk[1], xb[0][:, i * 128:(i + 1) * 128], ident[0], True, True,
                       [xb[1], ident[1]])
                dst = xnT[0].rearrange("p (i k) -> p i k", k=512)[:, half * 4:half * 4 + 4, t * 128:(t + 1) * 128]
                src = bk[0].rearrange("p (q k) -> p q k", k=128)
                evac(dst, xnT[1], src, bk[1])

    def norm_transpose(H, gain, xb_ring, xnT, msq_ring, junk, misc_banks):
        transpose_part(norm_part(H, gain, xb_ring, msq_ring, junk), xnT, misc_banks)

    def post_norm_add(src_halves, gain, Hap_t, Hb, msq_ring, junk, tsb):
        msq = msq_ring.next()
        for e_, (sa, sb_) in enumerate(src_halves):
            P.add("act", lambda e, sa=sa, e_=e_: e.activation(out=junk[0][:, 0:512], in_=sa, func=AF.Square, scale=1.0 / 32,
                                                              accum_out=msq[0][:, e_:e_ + 1]),
                  reads=[sb_], writes=[junk[1], msq[1]])
        rs = rstd_from(msq, 2)
        for e_, (sa, sb_) in enumerate(src_halves):
            P.add("dve", lambda e, sa=sa, e_=e_: e.scalar_tensor_tensor(out=tsb[0][:, e_ * 512:(e_ + 1) * 512], in0=sa, scalar=rs,
                                                                        in1=gain[0][:, e_ * 512:(e_ + 1) * 512],
                                                                        op0=ALU.mult, op1=ALU.mult),
                  reads=[sb_, msq[1], gain[1]], writes=[tsb[1]])
        P.add("pool", lambda e: e.tensor_tensor(out=Hap_t, in0=Hap_t, in1=tsb[0], op=ALU.add), reads=[tsb[1], Hb], writes=[Hb])

    def make_ffn_state():
        st = {}
        st["xb"] = A.ring(4, D, BF16)
        st["xnT"] = A.tile(8 * 512, BF16)
        st["msq"] = A.ring(4, 8, F32)
        st["junk"] = A.tile(D, BF16)
        st["tsb"] = A.tile(D, F32)
        st["wg"] = A.ring(3, 1024, BF16)
        st["wu"] = A.ring(3, 1024, BF16)
        st["wd"] = [A.tile(NJ * 512, BF16) for _ in range(2)]
        st["sg"] = A.ring(2, 512, F32)
        st["act"] = A.tile(NJ * 512, BF16)
        st["fsb"] = A.ring(1, D, F32)
        st["gbanks"] = Ring([banks[0], banks[1]])
        st["ubanks"] = Ring([banks[2], banks[3]])
        st["dbanks"] = Ring([banks[4], banks[5]])
        st["misc"] = Ring([banks[6], banks[7]])
        st["mbanks"] = Ring(banks[0:6])
        return st

    def ffn_wload(st, l, j):
        r = (l * NJ + j) * 128
        wg_t = st["wg"].next(); wu_t = st["wu"].next()
        P.add("sp", lambda e, wg_t=wg_t, r=r: e.dma_start(out=wg_t[0], in_=wb["wg"][r:r + 128, :]),
              reads=wdeps("wg", r, r + 128), writes=[wg_t[1]], dma=True)
        P.add("sp", lambda e, wu_t=wu_t, r=r: e.dma_start(out=wu_t[0], in_=wb["wu"][r:r + 128, :]),
              reads=wdeps("wu", r, r + 128), writes=[wu_t[1]], dma=True)
        return (wg_t, wu_t)

    def ffn_gateup(st, l, prefetch_next, hook=None):
        xnT, act_t = st["xnT"], st["act"]
        pref = st.setdefault("pref", [])
        for j in range(NJ):
            if j == 6 and hook is not None:
                hook()
            wg_t, wu_t = pref.pop(0) if pref else ffn_wload(st, l, j)
            gb = st["gbanks"].next(); ub = st["ubanks"].next()
            for i in range(8):
                mm(gb[0], gb[1], wg_t[0][:, i * 128:(i + 1) * 128], xnT[0][:, i * 512:(i + 1) * 512], i == 0, i == 7,
                   [wg_t[1], xnT[1]])
            for i in range(8):
                mm(ub[0], ub[1], wu_t[0][:, i * 128:(i + 1) * 128], xnT[0][:, i * 512:(i + 1) * 512], i == 0, i == 7,
                   [wu_t[1], xnT[1]])
            sg = st["sg"].next()
            P.add("act", lambda e, sg=sg, gb=gb: e.activation(out=sg[0], in_=gb[0], func=AF.Silu), reads=[gb[1]], writes=[sg[1]])
            P.add("dve", lambda e, sg=sg, ub=ub, j=j: e.tensor_tensor(out=act_t[0][:, j * 512:(j + 1) * 512], in0=sg[0], in1=ub[0],
                                                                      op=ALU.mult),
                  reads=[sg[1], ub[1]], writes=[act_t[1]])
        for e_ in range(2):
            r = (l * 2 + e_) * 128
            wd_t = st["wd"][e_]
            P.add("sp", lambda e, wd_t=wd_t, r=r: e.dma_start(out=wd_t[0], in_=wb["wd"][r:r + 128, :]),
                  reads=wdeps("wd", r, r + 128), writes=[wd_t[1]], dma=True)
        if prefetch_next:
            for j in range(3):
                pref.append(ffn_wload(st, l, j))

    def ffn_down(st, H, g_post, ts):
        Hap, Hb = H
        act_t = st["act"]
        for t in ts:
            fsb = st["fsb"].next()
            for e_ in range(2):
                db = st["dbanks"].next()
                wd_t = st["wd"][e_]
                for j in range(NJ):
                    mm(db[0], db[1], act_t[0][:, j * 512 + t * 128:j * 512 + (t + 1) * 128], wd_t[0][:, j * 512:(j + 1) * 512],
                       j == 0, j == NJ - 1, [act_t[1], wd_t[1]])
                evac(fsb[0][:, e_ * 512:(e_ + 1) * 512], fsb[1], db[0], db[1])
            post_norm_add([(fsb[0][:, 0:512], fsb[1]), (fsb[0][:, 512:1024], fsb[1])], g_post,
                          Hap[:, t * D:(t + 1) * D], Hb, st["msq"], st["junk"], st["tsb"])

    def pipelined_blocks(st, l, f_loads, f_compute, g_pre, g_post, store):
        H = f_compute(f_loads(0))
        transpose_part(norm_part(H, g_pre, st["xb"], st["msq"], st["junk"]), st["xnT"], st["misc"])
        for blk in range(16):
            pump(6)
            nxt = []
            hook = (lambda blk=blk, nxt=nxt: nxt.append(f_loads(blk + 1))) if blk + 1 < 16 else None
            ffn_gateup(st, l, blk + 1 < 16, hook)
            if blk + 1 < 16:
                Hn = f_compute(nxt[0])
                xbs = norm_part(Hn, g_pre, st["xb"], st["msq"], st["junk"])
            ffn_down(st, H, g_post, (0, 1))
            if blk + 1 < 16:
                transpose_part(xbs, st["xnT"], st["misc"])
            ffn_down(st, H, g_post, (2, 3))
            store(H, blk)
            if blk + 1 < 16:
                H = Hn

    if do0:
        A.off = persist_mark
        ga = A.tile(64 * 2 * 128, BF16)
        P.add("sp", lambda e: e.dma_start(out=ga[0][:, 0:8192], in_=ga_d[:, 0:8192]), writes=[ga[1]], dma=True)
        P.add("sp", lambda e: e.dma_start(out=ga[0][:, 8192:16384], in_=ga_d[:, 8192:16384]), writes=[ga[1]], dma=True)
        gpre0 = load_gain(g_mix_pre, 0)
        xr = A.ring(3, D, F32)
        xbr = A.ring(2, D, BF16)
        msqr = A.ring(4, 8, F32)
        junk = A.tile(D, BF16)
        arr = A.ring(2, D, BF16); air = A.ring(2, D, BF16)
        x3 = x.rearrange("(a b) c -> a b c", b=128)
        bankr = Ring(banks)

        def a_load(j):
            xt = xr.next()
            for e_ in range(2):
                src = x3[:, 2 * j + e_, :]
                P.add("sp", lambda e, xt=xt, e_=e_, src=src: e.dma_start(out=xt[0][e_ * 64:(e_ + 1) * 64, :], in_=src),
                      writes=[xt[1]], dma=True)
            return xt

        n_rest = (len(cast_q) + 8) - n_l0_casts if do1 else 0
        n_rest = max(0, min(n_rest, len(cast_q)))
        pend = [a_load(0), a_load(1)]
        for j in range(64):
            xt = pend.pop(0)
            if len(cast_q) > n_rest:
                pump(min(3, len(cast_q) - n_rest))
            if j + 2 < 64:
                pend.append(a_load(j + 2))
            msq = msqr.next()
            P.add("act", lambda e, xt=xt, msq=msq, junk=junk: e.activation(out=junk[0], in_=xt[0], func=AF.Square, scale=1.0 / 32,
                                                                           accum_out=msq[0][:, 0:1]),
                  reads=[xt[1]], writes=[junk[1], msq[1]])
            rs = rstd_from(msq, 1)
            xb = xbr.next()
            P.add("dve", lambda e, xt=xt, rs=rs, xb=xb: e.scalar_tensor_tensor(out=xb[0], in0=xt[0], scalar=rs, in1=gpre0[0],
                                                                               op0=ALU.mult, op1=ALU.mult),
                  reads=[xt[1], msq[1], gpre0[1]], writes=[xb[1]])
            art = arr.next(); ait = air.next()
            for tt, dstt in ((0, art), (1, ait)):
                for half in range(2):
                    bk = bankr.next()
                    lo = (j * 2 + tt) * 128
                    mm(bk[0], bk[1], ga[0][:, lo:lo + 128], xb[0][:, half * 512:(half + 1) * 512], True, True, [ga[1], xb[1]])
                    evac(dstt[0][:, half * 512:(half + 1) * 512], dstt[1], bk[0], bk[1])
            for dstt, dd in ((art, ar_d), (ait, ai_d)):
                for e_ in range(2):
                    dst = dd[:, 2 * j + e_, :]
                    P.add("sp", lambda e, dstt=dstt, e_=e_, dst=dst: e.dma_start(out=dst, in_=dstt[0][e_ * 64:(e_ + 1) * 64, :]),
                          reads=[dstt[1]], writes=[a_bufs[j]], dma=True)

        P.barrier()
        A.off = persist_mark
        gpost0 = load_gain(g_mix_post, 0)
        gfpre0 = load_gain(g_ffn_pre, 0)
        gfpost0 = load_gain(g_ffn_post, 0)
        fc = A.tile(512, BF16)
        ch = A.tile(256, BF16)
        P.add("sp", lambda e: e.dma_start(out=fc[0], in_=fc_d), writes=[fc[1]], dma=True)
        P.add("sp", lambda e: e.dma_start(out=ch[0], in_=ch_d), writes=[ch[1]], dma=True)
        wcs = A.tile(2 * 8 * 1024, BF16)
        st = make_ffn_state()
        Hr = A.ring(2, 4 * D, F32)
        a_r = A.ring(4, D, BF16); a_i = A.ring(4, D, BF16)
        ut = A.ring(1, 8 * 256, BF16)
        wf_t = st["wd"][0]
        P.add("sp", lambda e: e.dma_start(out=wf_t[0][:, 0:8192], in_=wb["wf"]), reads=wdeps("wf", 0, 128), writes=[wf_t[1]],
              dma=True)
        for g_ in range(8):
            for cs in range(2):
                for half in range(2):
                    bk = st["misc"].next()
                    mm(bk[0], bk[1], ch[0][:, cs * 128:(cs + 1) * 128],
                       wf_t[0][:, g_ * 1024 + half * 512:g_ * 1024 + (half + 1) * 512], True, True, [ch[1], wf_t[1]])
                    lo = cs * 8192 + g_ * 1024 + half * 512
                    evac(wcs[0][:, lo:lo + 512], wcs[1], bk[0], bk[1])
        x_r = x.rearrange("(b a) c -> a b c", a=64)
        h1_r = h1.rearrange("(b a) c -> a b c", a=64)
        def loads_b(blk):
            H = Hr.next()
            for t in range(4):
                k1 = blk * 4 + t
                src = x_r[k1]
                P.add("sp", lambda e, H=H, t=t, src=src: e.dma_start(out=H[0][:, t * D:(t + 1) * D], in_=src), writes=[H[1]], dma=True)
            tiles = []
            for t in range(4):
                k1 = blk * 4 + t
                art = a_r.next(); ait = a_i.next()
                P.add("sp", lambda e, art=art, k1=k1: e.dma_start(out=art[0], in_=ar_d[k1]), reads=a_bufs, writes=[art[1]], dma=True)
                P.add("sp", lambda e, ait=ait, k1=k1: e.dma_start(out=ait[0], in_=ai_d[k1]), reads=a_bufs, writes=[ait[1]], dma=True)
                tiles.append((art, ait))
            return (H, tiles)

        def front_b(ctx):
            H, tiles = ctx
            for t in range(4):
                art, ait = tiles[t]
                utt = ut.next()
                for pair in range(4):
                    bk = st["misc"].next()
                    for q in range(2):
                        i = pair * 2 + q
                        o_ = bk[0][:, q * 256:(q + 1) * 256]
                        mm(o_, bk[1], art[0][:, i * 128:(i + 1) * 128], fc[0][:, 0:256], True, False, [art[1], fc[1]])
                        mm(o_, bk[1], ait[0][:, i * 128:(i + 1) * 128], fc[0][:, 256:512], False, True, [ait[1], fc[1]])
                    evac(utt[0][:, pair * 512:(pair + 1) * 512], utt[1], bk[0], bk[1])
                mh = []
                for half in range(2):
                    db = st["mbanks"].next()
                    n_ = 0
                    for i in range(8):
                        for cs in range(2):
                            mm(db[0], db[1], utt[0][:, i * 256 + cs * 128:i * 256 + (cs + 1) * 128],
                               wcs[0][:, cs * 8192 + i * 1024 + half * 512:cs * 8192 + i * 1024 + (half + 1) * 512],
                               n_ == 0, n_ == 15, [utt[1], wcs[1]])
                            n_ += 1
                    mh.append(db)
                post_norm_add(mh, gpost0, H[0][:, t * D:(t + 1) * D], H[1], st["msq"], st["junk"], st["tsb"])
            return H

        def store_b(H, blk):
            for t in range(4):
                k1 = blk * 4 + t
                dst = h1_r[k1]
                P.add("sp", lambda e, H=H, t=t, dst=dst: e.dma_start(out=dst, in_=H[0][:, t * D:(t + 1) * D]),
                      reads=[H[1]], writes=[h1_bufs[blk]], dma=True)

        pipelined_blocks(st, 0, loads_b, front_b, gfpre0, gfpost0, store_b)

    if do1:
        pump(10 ** 6)
        P.barrier()
        A.off = persist_mark
        gpre1 = load_gain(g_mix_pre, 1)
        wqk_t = A.tile(16 * 1024, BF16)
        wv_t = A.tile(8192, BF16)
        for m in range(16):
            P.add("sp", lambda e, m=m: e.dma_start(out=wqk_t[0][:, m * 1024:(m + 1) * 1024], in_=wb["wqk"][m * 128:(m + 1) * 128, :]),
                  reads=wdeps("wqk", m * 128, (m + 1) * 128), writes=[wqk_t[1]], dma=True)
        P.add("sp", lambda e: e.dma_start(out=wv_t[0], in_=wb["wv"]), reads=wdeps("wv", 0, 128), writes=[wv_t[1]], dma=True)
        Hr = A.ring(2, 4 * D, F32)
        xbr = A.ring(4, D, BF16)
        xnT = A.tile(8 * 512, BF16)
        msqr = A.ring(4, 8, F32)
        junk = A.tile(D, BF16)
        qko = A.ring(4, 512, BF16)
        vo = A.ring(2, D, BF16)
        misc = Ring([banks[6], banks[7]])
        pb = Ring(banks[0:6])
        h1_t = h1.rearrange("(n p) c -> n p c", p=128)
        v_w = v_d.rearrange("h k t e -> t k h e")
        all_h1 = h1_bufs if do0 else []
        def q_loads(blk):
            H = Hr.next()
            for t in range(4):
                P.add("sp", lambda e, H=H, t=t, blk=blk: e.dma_start(out=H[0][:, t * D:(t + 1) * D], in_=h1_t[blk * 4 + t]),
                      reads=all_h1, writes=[H[1]], dma=True)
            return H

        Hq = q_loads(0)
        xbs = norm_part(Hq, gpre1, xbr, msqr, junk)
        transpose_part(xbs, xnT, misc)
        for blk in range(16):
            if blk + 1 < 16:
                Hq = q_loads(blk + 1)
                xbs = norm_part(Hq, gpre1, xbr, msqr, junk)
            for m in range(16):
                bk = pb.next()
                for i in range(8):
                    mm(bk[0], bk[1], wqk_t[0][:, m * 1024 + i * 128:m * 1024 + (i + 1) * 128], xnT[0][:, i * 512:(i + 1) * 512],
                       i == 0, i == 7, [wqk_t[1], xnT[1]])
                qk = qko.next()
                evac(qk[0], qk[1], bk[0], bk[1])
                dst = (qT_d if m < 8 else kT_d)[m % 8, :, blk * 512:(blk + 1) * 512]
                P.add("sp", lambda e, qk=qk, dst=dst: e.dma_start(out=dst, in_=qk[0]), reads=[qk[1]], writes=[qkv_bufs[blk]], dma=True)
            for t in range(4):
                vt = vo.next()
                for half in range(2):
                    bk = pb.next()
                    for i in range(8):
                        mm(bk[0], bk[1], xnT[0][:, i * 512 + t * 128:i * 512 + (t + 1) * 128],
                           wv_t[0][:, i * 1024 + half * 512:i * 1024 + (half + 1) * 512], i == 0, i == 7, [xnT[1], wv_t[1]])
                    evac(vt[0][:, half * 512:(half + 1) * 512], vt[1], bk[0], bk[1])
                dst = v_w[blk * 4 + t]
                P.add("sp", lambda e, vt=vt, dst=dst: e.dma_start(out=dst, in_=vt[0].rearrange("p (h e) -> p h e", e=128)),
                      reads=[vt[1]], writes=[qkv_bufs[blk]], dma=True)
            if blk + 1 < 16:
                transpose_part(xbs, xnT, misc)

        P.barrier()
        A.off = persist_mark
        q1 = A.tile(512, F32); t1 = A.tile(2048, F32); bct = A.tile(NH * 128, F32)
        P.add("sp", lambda e: e.dma_start(out=q1[0], in_=q1_d), writes=[q1[1]], dma=True)
        P.add("sp", lambda e: e.dma_start(out=t1[0], in_=t1_d), writes=[t1[1]], dma=True)
        P.add("sp", lambda e: e.dma_start(out=bct[0], in_=bc_d), writes=[bct[1]], dma=True)
        lamt = A.tile(4 * 64, F32); lamw = A.tile(2 * 64, F32); lams = A.tile(8, F32)
        for n_, ap_ in enumerate(lam_in):
            P.add("sp", lambda e, n_=n_, ap_=ap_: e.dma_start(out=lamt[0][:, n_ * 64:(n_ + 1) * 64], in_=ap_.partition_broadcast(128)),
                  writes=[lamt[1]], dma=True)
        for n_ in range(2):
            P.add("dve", lambda e, n_=n_: e.tensor_tensor(out=lamw[0][:, n_ * 64:(n_ + 1) * 64], in0=lamt[0][:, n_ * 128:n_ * 128 + 64],
                                                          in1=lamt[0][:, n_ * 128 + 64:n_ * 128 + 128], op=ALU.mult),
                  reads=[lamt[1]], writes=[lamw[1]])
            P.add("act", lambda e, n_=n_: e.activation(out=lamw[0][:, n_ * 64:(n_ + 1) * 64], in_=lamw[0][:, n_ * 64:(n_ + 1) * 64],
                                                       func=AF.Copy, accum_out=lams[0][:, n_:n_ + 1]),
                  reads=[lamw[1]], writes=[lamw[1], lams[1]])
        P.add("act", lambda e: e.activation(out=lams[0][:, 2:4], in_=lams[0][:, 0:2], func=AF.Exp), reads=[lams[1]], writes=[lams[1]])
        P.add("dve", lambda e: e.tensor_tensor(out=lams[0][:, 4:5], in0=lams[0][:, 3:4], in1=lams[0][:, 2:3], op=ALU.subtract),
              reads=[lams[1]], writes=[lams[1]])
        P.add("dve", lambda e: e.tensor_scalar(out=lams[0][:, 5:6], in0=lams[0][:, 4:5], scalar1=-LAM_INIT, scalar2=None, op0=ALU.add),
              reads=[lams[1]], writes=[lams[1]])
        neg_lam = lams[0][:, 5:6]
        gsub = A.tile(128, F32)
        P.add("sp", lambda e: e.dma_start(out=gsub[0], in_=subln.partition_broadcast(128)), writes=[gsub[1]], dma=True)
        P.add("dve", lambda e: e.tensor_scalar(out=gsub[0], in0=gsub[0], scalar1=1.0 - LAM_INIT, scalar2=None, op0=ALU.mult),
              reads=[gsub[1]], writes=[gsub[1]])
        qT_r = Ring([(A.tile(S, BF16), A.tile(S, BF16)) for _ in range(2)])
        for qp0, qp1 in qT_r.items:
            P.add("pool", lambda e, qp0=qp0: e.memset(qp0[0][64:128, :], 0.0), writes=[qp0[1]])
            P.add("pool", lambda e, qp1=qp1: e.memset(qp1[0][0:64, :], 0.0), writes=[qp1[1]])
        kT_r = A.ring(2, S, BF16); v_r = A.ring(2, 64 * VP, BF16)
        osb_r = A.ring(2, 8 * 132, F32)
        for it in v_r.items:
            v3 = it[0].rearrange("p (t e) -> p t e", e=VP)
            P.add("pool", lambda e, v3=v3: e.memset(v3[:, :, 128:129], 1.0), writes=[it[1]])
        tmp_r = A.ring(5, 512, F32)
        E_r = A.ring(6, 512, BF16)
        ep = A.ring(2, 16, F32)
        o1_r = A.ring(2, 512, F32); o_r = A.ring(2, 512, F32)
        msqr = A.ring(4, 8, F32)
        junk = A.tile(128, BF16)
        ot_r = A.ring(2, 512, BF16)
        st_banks = Ring(banks[0:5])
        o_dv = o_d.rearrange("(b j p) (h e) -> b h p j e", j=4, p=128, e=128)
        deferred = []
        for h in range(NH):
            sl = SLOPES[h]
            qP = qT_r.next(); kT = kT_r.next(); vv = v_r.next()
            for c_ in range(2):
                P.add("sp", lambda e, qP=qP, h=h, c_=c_: e.dma_start(out=qP[c_][0][c_ * 64:(c_ + 1) * 64, :],
                                                                   in_=qT_d[h, c_ * 64:(c_ + 1) * 64, :]),
                      reads=qkv_bufs, writes=[qP[c_][1]], dma=True)
            P.add("sp", lambda e, kT=kT, h=h: e.dma_start(out=kT[0], in_=kT_d[h]), reads=qkv_bufs, writes=[kT[1]], dma=True)
            v3 = vv[0].rearrange("p (t e) -> p t e", e=VP)
            for q4 in range(4):
                P.add("sp", lambda e, v3=v3, h=h, q4=q4: e.dma_start(out=v3[:, q4 * 16:(q4 + 1) * 16, 0:128],
                                                                     in_=v_d[h, :, q4 * 16:(q4 + 1) * 16, :]),
                      reads=qkv_bufs, writes=[vv[1]], dma=True)
            units = []
            for qb in range(16):
                pl = []
                for kt in range(64):
                    n = kt - 4 * qb
                    if 0 <= n <= 3:
                        kind = ("D", n)
                    elif n >= 4:
                        if sl * (128 * n - 511) > ATT_THRESH:
                            continue
                        kind = ("R", n)
                    else:
                        if sl * (128 * (-n) - 127) > ATT_THRESH:
                            continue
                        kind = ("L", -n)
                    for comp in range(2):
                        pl.append([qb, kt, comp, kind, False, False])
                pl[0][4] = True
                pl[-1][5] = True
                units.extend(pl)
            nu = len(units)
            stb = {}

            def issue_st(i):
                qb_, kt_, comp_ = units[i][0:3]
                bk = st_banks.next()
                stb[i] = bk
                mm(bk[0], bk[1], kT[0][:, kt_ * 128:(kt_ + 1) * 128],
                   qP[comp_][0][:, qb_ * 512:(qb_ + 1) * 512], True, True, [kT[1], qP[comp_][1]])

            for i in range(min(LOOKAHEAD, nu)):
                issue_st(i)
            first_in_bank = {}
            for i in range(nu):
                if i + LOOKAHEAD < nu:
                    issue_st(i + LOOKAHEAD)
                qb, kt, comp, kind, is_first, is_last = units[i]
                if is_first:
                    first_in_bank = {}
                bk = stb.pop(i)
                tmp = tmp_r.next()
                if kind[0] == "D":
                    in1 = t1[0][:, kind[1] * 512:(kind[1] + 1) * 512]; in1b = t1[1]; op1 = ALU.add
                    bias = 0.0; br = []
                elif kind[0] == "R":
                    in1 = q1[0]; in1b = q1[1]; op1 = ALU.add
                    c_ = h * 128 + kind[1]
                    bias = bct[0][:, c_:c_ + 1]; br = [bct[1]]
                else:
                    in1 = q1[0]; in1b = q1[1]; op1 = ALU.subtract
                    c_ = h * 128 + 64 + kind[1]
                    bias = bct[0][:, c_:c_ + 1]; br = [bct[1]]
                P.add("dve", lambda e, tmp=tmp, bk=bk, in1=in1, op1=op1, sl=sl: e.scalar_tensor_tensor(
                    out=tmp[0], in0=bk[0], scalar=0.125 / sl, in1=in1, op0=ALU.mult, op1=op1),
                    reads=[bk[1], in1b], writes=[tmp[1]])
                E = E_r.next()
                P.add("act", lambda e, E=E, tmp=tmp, bias=bias, sl=sl: e.activation(out=E[0], in_=tmp[0], func=AF.Exp, bias=bias, scale=sl),
                      reads=[tmp[1]] + br, writes=[E[1]])
                for jq in range(4):
                    g_ = comp * 4 + jq
                    bkey = 5 + g_ // 3
                    ob = banks[bkey]
                    oa = ob[0][:, (g_ % 3) * 160:(g_ % 3) * 160 + 129]
                    start = bkey not in first_in_bank
                    first_in_bank[bkey] = True
                    mm(oa, ob[1], E[0][:, jq * 128:(jq + 1) * 128], v3[:, kt, 0:129], start, is_last or (i + 1 < nu and units[i + 1][5]),
                       [E[1], vv[1]], skip=True)
                if deferred and not is_last:
                    f_ = deferred.pop(0)
                    if f_ is not None:
                        f_()
                if not is_last:
                    continue
                osb = osb_r.next()
                osb4 = osb[0].rearrange("p (a c) -> p a c", c=132)
                for bi in range(3):
                    ng = 3 if bi < 2 else 2
                    evac(osb4[:, 3 * bi:3 * bi + ng, 0:129], osb[1],
                         banks[5 + bi][0][:, 0:480].rearrange("p (g c) -> p g c", c=160)[:, 0:ng, 0:129], banks[5 + bi][1])
                ott = ot_r.next()
                epp = ep.next()
                o1 = o1_r.next(); oo = o_r.next(); msq = msqr.next()

                def s1(osb=osb, osb4=osb4, epp=epp):
                    e3 = epp[0][:, 0:12].rearrange("p (a o) -> p a o", o=1)
                    P.add("dve", lambda e: e.reciprocal(out=e3[:, 0:4, :], in_=osb4[:, 4:8, 128:129]), reads=[osb[1]], writes=[epp[1]])
                    P.add("dve", lambda e: e.tensor_tensor(out=e3[:, 4:8, :], in0=e3[:, 0:4, :], in1=osb4[:, 0:4, 128:129], op=ALU.mult),
                          reads=[osb[1], epp[1]], writes=[epp[1]])
                    P.add("dve", lambda e: e.tensor_scalar(out=epp[0][:, 8:12], in0=epp[0][:, 4:8], scalar1=neg_lam, scalar2=None,
                                                           op0=ALU.mult),
                          reads=[epp[1], lams[1]], writes=[epp[1]])

                def s2(osb=osb, epp=epp, oo=oo):
                    for jq in range(4):
                        P.add("dve", lambda e, jq=jq: e.scalar_tensor_tensor(
                            out=oo[0][:, jq * 128:(jq + 1) * 128], in0=osb[0][:, (4 + jq) * 132:(4 + jq) * 132 + 128],
                            scalar=epp[0][:, 8 + jq:9 + jq], in1=osb[0][:, jq * 132:jq * 132 + 128], op0=ALU.mult, op1=ALU.add),
                            reads=[osb[1], epp[1]], writes=[oo[1]])

                def s3(oo=oo, msq=msq):
                    for jq in range(4):
                        P.add("act", lambda e, jq=jq: e.activation(out=junk[0], in_=oo[0][:, jq * 128:(jq + 1) * 128], func=AF.Square,
                                                                   scale=128.0 ** -0.5, accum_out=msq[0][:, jq:jq + 1]),
                              reads=[oo[1]], writes=[junk[1], msq[1]])

                def s4(msq=msq, osb=osb, osb4=osb4):
                    m3 = msq[0][:, 0:8].rearrange("p (a o) -> p a o", o=1)
                    P.add("pool", lambda e: e.tensor_tensor(out=m3[:, 4:8, :], in0=osb4[:, 0:4, 128:129], in1=osb4[:, 0:4, 128:129], op=ALU.mult),
                          reads=[osb[1]], writes=[msq[1]])
                    P.add("pool", lambda e: e.tensor_tensor(out=msq[0][:, 4:8], in0=msq[0][:, 4:8], in1=cst[0][:, 0:4], op=ALU.mult),
                          reads=[msq[1], cst[1]], writes=[msq[1]])
                    P.add("pool", lambda e: e.tensor_tensor(out=msq[0][:, 4:8], in0=msq[0][:, 4:8], in1=msq[0][:, 0:4], op=ALU.add),
                          reads=[msq[1]], writes=[msq[1]])
                    P.add("pool", lambda e: e.tensor_tensor(out=msq[0][:, 4:8], in0=msq[0][:, 4:8], in1=cst[0][:, 4:8], op=ALU.pow),
                          reads=[msq[1], cst[1]], writes=[msq[1]])

                def s5():
                    pass

                def s6(oo=oo, msq=msq, ott=ott, qb=qb, h=h):
                    for jq in range(4):
                        P.add("dve", lambda e, jq=jq: e.scalar_tensor_tensor(
                            out=ott[0][:, jq * 128:(jq + 1) * 128], in0=oo[0][:, jq * 128:(jq + 1) * 128], scalar=msq[0][:, 4 + jq:5 + jq],
                            in1=gsub[0], op0=ALU.mult, op1=ALU.mult),
                            reads=[oo[1], msq[1], gsub[1]], writes=[ott[1]])
                    dst = o_dv[qb, h]
                    P.add("sp", lambda e: e.dma_start(out=dst, in_=ott[0].rearrange("p (j e) -> p j e", e=128)),
                          reads=[ott[1]], writes=[o_bufs[qb]], dma=True)

                while deferred:
                    f_ = deferred.pop(0)
                    if f_ is not None:
                        f_()
                deferred.extend([s1, None, s2, s3, None, s4, None, None, s6])
        while deferred:
            f_ = deferred.pop(0)
            if f_ is not None:
                f_()

        P.barrier()
        A.off = persist_mark
        gpost1 = load_gain(g_mix_post, 1)
        gfpre1 = load_gain(g_ffn_pre, 1)
        gfpost1 = load_gain(g_ffn_post, 1)
        wo_t = A.tile(8192, BF16)
        P.add("sp", lambda e: e.dma_start(out=wo_t[0], in_=wb["wo"]), reads=wdeps("wo", 0, 128), writes=[wo_t[1]], dma=True)
        st = make_ffn_state()
        Hr = A.ring(2, 4 * D, F32)
        ob_r = A.ring(4, D, BF16)
        oT = A.tile(8 * 512, BF16)
        o_t = o_d.rearrange("(n p) c -> n p c", p=128)
        out_t = out.rearrange("(n p) c -> n p c", p=128)
        all_h1 = h1_bufs if do0 else []
        def loads_u(blk):
            H = Hr.next()
            for t in range(4):
                P.add("sp", lambda e, H=H, t=t, blk=blk: e.dma_start(out=H[0][:, t * D:(t + 1) * D], in_=h1_t[blk * 4 + t]),
                      reads=all_h1, writes=[H[1]], dma=True)
            tiles = []
            for t in range(4):
                obt = ob_r.next()
                P.add("sp", lambda e, obt=obt, t=t, blk=blk: e.dma_start(out=obt[0], in_=o_t[blk * 4 + t]), reads=o_bufs,
                      writes=[obt[1]], dma=True)
                tiles.append(obt)
            return (H, tiles)

        def front_u(ctx):
            H, tiles = ctx
            for t in range(4):
                obt = tiles[t]
                for half in range(2):
                    bk = st["misc"].next()
                    for q in range(4):
                        i = half * 4 + q
                        mm(bk[0][:, q * 128:(q + 1) * 128], bk[1], obt[0][:, i * 128:(i + 1) * 128], ident[0], True, True,
                           [obt[1], ident[1]])
                    dst = oT[0].rearrange("p (i k) -> p i k", k=512)[:, half * 4:half * 4 + 4, t * 128:(t + 1) * 128]
                    evac(dst, oT[1], bk[0].rearrange("p (q k) -> p q k", k=128), bk[1])
            for t in range(4):
                mh = []
                for half in range(2):
                    db = st["mbanks"].next()
                    for i in range(8):
                        mm(db[0], db[1], oT[0][:, i * 512 + t * 128:i * 512 + (t + 1) * 128],
                           wo_t[0][:, i * 1024 + half * 512:i * 1024 + (half + 1) * 512], i == 0, i == 7, [oT[1], wo_t[1]])
                    mh.append(db)
                post_norm_add(mh, gpost1, H[0][:, t * D:(t + 1) * D], H[1], st["msq"], st["junk"], st["tsb"])
            return H

        def store_u(H, blk):
            for t in range(4):
                P.add("sp", lambda e, H=H, t=t, blk=blk: e.dma_start(out=out_t[blk * 4 + t], in_=H[0][:, t * D:(t + 1) * D]),
                      reads=[H[1]], dma=True)

        pipelined_blocks(st, 1, loads_u, front_u, gfpre1, gfpost1, store_u)

    P.emit()
    return nc


_PROG_CACHE = {}


def _common_inputs(inp):
    global _CONSTS
    if _CONSTS is None:
        _CONSTS = _host_consts()
    m = dict(_CONSTS)
    m.update(_layout_weights(inp))
    for k in ("norm_mix_pre", "norm_mix_post", "norm_ffn_pre", "norm_ffn_post", "diff_lambda_q1", "diff_lambda_k1",
              "diff_lambda_q2", "diff_lambda_k2", "diff_subln_g"):
        m[k] = np.ascontiguousarray(np.asarray(inp[k], dtype=np.float32))
    return m


L0_KEYS = ("norm_mix_pre", "norm_mix_post", "norm_ffn_pre", "norm_ffn_post", "diff_lambda_q1", "diff_lambda_k1",
           "diff_lambda_q2", "diff_lambda_k2", "diff_subln_g", "ident", "ga", "fc", "ch", "q1", "t1", "bc",
           "wg", "wu", "wd", "wqk", "wv", "wo", "wf")

FUSED = True


def kernel(**inputs):
    x = np.ascontiguousarray(np.asarray(inputs["x"], dtype=np.float32))
    common = _common_inputs(inputs)
    if FUSED:
        nc = build("full")
        in_maps = [dict(common, x=x[b]) for b in range(8)]
        res = run_bass_kernel_spmd(nc, in_maps, core_ids=list(range(8)))
        return np.stack([r["out"] for r in res.results], axis=0)
    nc0 = build("L0")
    res0 = run_bass_kernel_spmd(nc0, [dict(common, x=x[b]) for b in range(8)], core_ids=list(range(8)))
    h1 = [r["h1"] for r in res0.results]
    nc1 = build("L1")
    res1 = run_bass_kernel_spmd(nc1, [dict(common, h1=h1[b]) for b in range(8)], core_ids=list(range(8)))
    return np.stack([r["out"] for r in res1.results], axis=0)
```
